# Optimizing a Trainium2 kernel written in Bass

```python
import math
import jax, jax.numpy as jnp
from jax import lax
import numpy as np

D_MODEL = 1024
BATCH = 8
SEQ = 2048
DEPTH = 4

N_MIXERS = 2
N_MOBA_LAYERS = (DEPTH + 1) // 2
N_DIFF_LAYERS = DEPTH // 2
MOBA_HEAD_DIM = 64
MOBA_HEADS = D_MODEL // MOBA_HEAD_DIM
MOBA_BLOCK = 256
MOBA_TOPK = 3
MOBA_Q_CHUNK = 16
DIFF_HEAD_DIM = 64
DIFF_HEADS = D_MODEL // (2 * DIFF_HEAD_DIM)
DIFF_Q_BLOCK = 128
D_FF = 4 * D_MODEL
ROPE_THETA = 500000.0
MOBA_ROT = MOBA_HEAD_DIM // 4
DIFF_ROT = DIFF_HEAD_DIM // 4
DEEPNORM_ALPHA = (2.0 * DEPTH) ** 0.25
DEEPNORM_BETA = (8.0 * DEPTH) ** -0.25
LN_EPS = 1e-5
SUBLN_EPS = 1e-5
ADA_SCALE = 0.1
NEG_INF = -1e30

kernel_name = "moba_diffattn_deepnorm_adaln_hybrid"


def rope_tables(seq_len, rot_dim):
    pos = jnp.arange(seq_len, dtype=jnp.float32)
    inv = ROPE_THETA ** (-jnp.arange(0, rot_dim, 2, dtype=jnp.float32) / rot_dim)
    ang = pos[:, None] * inv[None, :]
    return jnp.cos(ang), jnp.sin(ang)


def partial_rope(x, cos, sin):
    half = cos.shape[-1]
    x1, x2, xp = x[..., :half], x[..., half:2 * half], x[..., 2 * half:]
    c = cos.astype(x.dtype)
    s = sin.astype(x.dtype)
    return jnp.concatenate([x1 * c - x2 * s, x2 * c + x1 * s, xp], axis=-1)


def layer_norm(x, g, b):
    xf = x.astype(jnp.float32)
    mu = jnp.mean(xf, axis=-1, keepdims=True)
    var = jnp.mean(jnp.square(xf - mu), axis=-1, keepdims=True)
    return ((xf - mu) * lax.rsqrt(var + LN_EPS)).astype(x.dtype) * g + b


def rms_norm(x, g):
    xf = x.astype(jnp.float32)
    return (xf * lax.rsqrt(jnp.mean(jnp.square(xf), axis=-1, keepdims=True) + SUBLN_EPS)).astype(x.dtype) * g


def moba_attention(h, w_in, w_out):
    B, S, _ = h.shape
    H, hd, BLK, QC = MOBA_HEADS, MOBA_HEAD_DIM, MOBA_BLOCK, MOBA_Q_CHUNK
    cos, sin = rope_tables(S, MOBA_ROT)
    q, k, v = jnp.split(h @ w_in, 3, axis=-1)
    to_heads = lambda t: t.reshape(B, S, H, hd).transpose(0, 2, 1, 3)
    q = partial_rope(to_heads(q), cos, sin)
    k = partial_rope(to_heads(k), cos, sin)
    v = to_heads(v)
    nb = -(-S // BLK)
    pad = nb * BLK - S
    k_blocks = jnp.pad(k, ((0, 0), (0, 0), (0, pad), (0, 0))).reshape(B, H, nb, BLK, hd)
    v_blocks = jnp.pad(v, ((0, 0), (0, 0), (0, pad), (0, 0))).reshape(B, H, nb, BLK, hd)
    scale = hd ** -0.5
    q_block = jnp.arange(S) // BLK
    n_sel = min(MOBA_TOPK, nb - 1)
    nc = S // QC

    def chunk(t):
        return jnp.moveaxis(t.reshape(t.shape[:2] + (nc, QC) + t.shape[3:]), 2, 0)

    xs = {"q": chunk(q), "start": jnp.arange(nc, dtype=jnp.int32) * QC}
    if n_sel > 0:
        k_mean = jnp.mean(k_blocks.astype(jnp.float32), axis=3)
        gate = jnp.einsum('bhsd,bhnd->bhsn', q.astype(jnp.float32), k_mean)
        past = jnp.arange(nb)[None, :] < q_block[:, None]
        gate = jnp.where(past, gate, NEG_INF)
        _, sel_idx = lax.top_k(gate, n_sel)
        sel_valid = jnp.arange(n_sel)[None, :] < q_block[:, None]
        xs["idx"] = chunk(sel_idx)
        xs["valid"] = sel_valid.reshape(nc, QC, n_sel)
    b_ix = jnp.arange(B)[:, None, None, None]
    h_ix = jnp.arange(H)[None, :, None, None]

    def step(a):
        q_c, start = a["q"], a["start"]
        own = start // BLK
        k_own = lax.dynamic_index_in_dim(k_blocks, own, axis=2, keepdims=False)
        v_own = lax.dynamic_index_in_dim(v_blocks, own, axis=2, keepdims=False)
        q_pos = start + jnp.arange(QC)
        k_pos = own * BLK + jnp.arange(BLK)
        s_own = jnp.einsum('bhqd,bhkd->bhqk', q_c, k_own).astype(jnp.float32) * scale
        s_own = jnp.where(k_pos[None, :] <= q_pos[:, None], s_own, NEG_INF)
        if n_sel == 0:
            p = jax.nn.softmax(s_own, axis=-1).astype(v.dtype)
            return jnp.einsum('bhqk,bhkd->bhqd', p, v_own)
        idx_c = a["idx"]
        k_sel = k_blocks[b_ix, h_ix, idx_c]
        v_sel = v_blocks[b_ix, h_ix, idx_c]
        s_sel = jnp.einsum('bhqd,bhqnkd->bhqnk', q_c, k_sel).astype(jnp.float32) * scale
        s_sel = jnp.where(a["valid"][None, None, :, :, None], s_sel, NEG_INF)
        s_all = jnp.concatenate([s_own, s_sel.reshape(B, H, QC, n_sel * BLK)], axis=-1)
        p = jax.nn.softmax(s_all, axis=-1).astype(v.dtype)
        p_own = p[..., :BLK]
        p_sel = p[..., BLK:].reshape(B, H, QC, n_sel, BLK)
        return (jnp.einsum('bhqk,bhkd->bhqd', p_own, v_own)
                + jnp.einsum('bhqnk,bhqnkd->bhqd', p_sel, v_sel))

    o = lax.map(step, xs)
    o = jnp.moveaxis(o, 0, 2).reshape(B, H, S, hd)
    return o.transpose(0, 2, 1, 3).reshape(B, S, H * hd) @ w_out


def diff_attention(h, w_in, w_out, lam_q1, lam_k1, lam_q2, lam_k2, subln_g, lambda_init):
    B, S, _ = h.shape
    H, d, QB = DIFF_HEADS, DIFF_HEAD_DIM, DIFF_Q_BLOCK
    cos, sin = rope_tables(S, DIFF_ROT)
    q, k, v = jnp.split(h @ w_in, 3, axis=-1)
    q = partial_rope(q.reshape(B, S, H, 2, d).transpose(0, 2, 3, 1, 4), cos, sin)
    k = partial_rope(k.reshape(B, S, H, 2, d).transpose(0, 2, 3, 1, 4), cos, sin)
    v = v.reshape(B, S, H, 2 * d).transpose(0, 2, 1, 3)
    f32 = jnp.float32
    lam = (jnp.exp(jnp.sum(lam_q1.astype(f32) * lam_k1.astype(f32)))
           - jnp.exp(jnp.sum(lam_q2.astype(f32) * lam_k2.astype(f32))) + lambda_init)
    nqb = S // QB
    q_blocks = jnp.moveaxis(q.reshape(B, H, 2, nqb, QB, d), 3, 0)
    k_pos = jnp.arange(S)
    scale = d ** -0.5

    def step(a):
        q_b, start = a
        s = jnp.einsum('bhcqd,bhckd->bhcqk', q_b, k).astype(f32) * scale
        q_pos = start + jnp.arange(QB)
        s = jnp.where(k_pos[None, :] <= q_pos[:, None], s, NEG_INF)
        p = jax.nn.softmax(s, axis=-1)
        w = (p[:, :, 0] - lam * p[:, :, 1]).astype(v.dtype)
        return jnp.einsum('bhqk,bhke->bhqe', w, v)

    o = lax.map(step, (q_blocks, jnp.arange(nqb, dtype=jnp.int32) * QB))
    o = jnp.moveaxis(o, 0, 2).reshape(B, H, S, 2 * d)
    o = rms_norm(o, subln_g) * (1.0 - lambda_init)
    return o.transpose(0, 2, 1, 3).reshape(B, S, H * 2 * d) @ w_out


def sq_relu_mlp(h, w_up, w_down):
    return jnp.square(jax.nn.relu(h @ w_up)) @ w_down


def setup_inputs(seed: int = 0) -> dict:
    key = jax.random.key(seed)
    ks = jax.random.split(key, 20)
    D = D_MODEL
    nrm = lambda k, shape: jax.random.normal(k, shape, dtype=jnp.float32)
    in_col_scale = jnp.concatenate([jnp.ones((2 * D,), jnp.float32),
                                    jnp.full((D,), DEEPNORM_BETA, jnp.float32)])
    return {
        "x": nrm(ks[0], (BATCH, SEQ, D)),
        "c": nrm(ks[1], (BATCH, D)),
        "moba_w_in": nrm(ks[2], (N_MOBA_LAYERS, D, 3 * D)) * D ** -0.5 * in_col_scale,
        "moba_w_out": nrm(ks[3], (N_MOBA_LAYERS, D, D)) * D ** -0.5 * DEEPNORM_BETA,
        "diff_w_in": nrm(ks[4], (N_DIFF_LAYERS, D, 3 * D)) * D ** -0.5 * in_col_scale,
        "diff_w_out": nrm(ks[5], (N_DIFF_LAYERS, D, D)) * D ** -0.5 * DEEPNORM_BETA,
        "diff_lam_q1": nrm(ks[6], (N_DIFF_LAYERS, DIFF_HEAD_DIM)) * 0.1,
        "diff_lam_k1": nrm(ks[7], (N_DIFF_LAYERS, DIFF_HEAD_DIM)) * 0.1,
        "diff_lam_q2": nrm(ks[8], (N_DIFF_LAYERS, DIFF_HEAD_DIM)) * 0.1,
        "diff_lam_k2": nrm(ks[9], (N_DIFF_LAYERS, DIFF_HEAD_DIM)) * 0.1,
        "diff_subln_g": 1.0 + 0.02 * nrm(ks[10], (N_DIFF_LAYERS, 2 * DIFF_HEAD_DIM)),
        "ada_w": nrm(ks[11], (DEPTH, D, 6 * D)) * D ** -0.5 * ADA_SCALE,
        "ada_b": 0.02 * nrm(ks[12], (DEPTH, 6 * D)),
        "ln_g": 1.0 + 0.02 * nrm(ks[13], (DEPTH, 2, D)),
        "ln_b": 0.02 * nrm(ks[14], (DEPTH, 2, D)),
        "mlp_w_up": nrm(ks[15], (DEPTH, D, D_FF)) * D ** -0.5,
        "mlp_w_down": nrm(ks[16], (DEPTH, D_FF, D)) * D_FF ** -0.5 * DEEPNORM_BETA,
    }


def reference(x, c, moba_w_in, moba_w_out, diff_w_in, diff_w_out, diff_lam_q1, diff_lam_k1,
              diff_lam_q2, diff_lam_k2, diff_subln_g, ada_w, ada_b, ln_g, ln_b, mlp_w_up, mlp_w_down):
    mod = jnp.einsum('bd,lde->lbe', jax.nn.silu(c), ada_w) + ada_b[:, None, :]
    for i in range(DEPTH):
        shift1, scale1, gate1, shift2, scale2, gate2 = [m[:, None, :] for m in jnp.split(mod[i], 6, axis=-1)]
        j = i // N_MIXERS
        h = x * (1.0 + scale1) + shift1
        if i % N_MIXERS == 0:
            y = moba_attention(h, moba_w_in[j], moba_w_out[j])
        else:
            lambda_init = 0.8 - 0.6 * math.exp(-0.3 * i)
            y = diff_attention(h, diff_w_in[j], diff_w_out[j], diff_lam_q1[j], diff_lam_k1[j],
                               diff_lam_q2[j], diff_lam_k2[j], diff_subln_g[j], lambda_init)
        x = layer_norm(DEEPNORM_ALPHA * x + (1.0 + gate1) * y, ln_g[i, 0], ln_b[i, 0])
        h = x * (1.0 + scale2) + shift2
        y = sq_relu_mlp(h, mlp_w_up[i], mlp_w_down[i])
        x = layer_norm(DEEPNORM_ALPHA * x + (1.0 + gate2) * y, ln_g[i, 1], ln_b[i, 1])
    return x
```

```python
import contextlib
import math

import numpy as np
import concourse.bass as bass
import concourse.mybir as mybir
from concourse.bass_utils import run_bass_kernel_spmd

F32 = mybir.dt.float32
BF16 = mybir.dt.bfloat16
ALU = mybir.AluOpType
AF = mybir.ActivationFunctionType
AX = mybir.AxisListType

D = 1024
S = 2048
DEPTH = 4
DFF = 4096
ALPHA = (2.0 * DEPTH) ** 0.25
NEG = -30000.0
LN_EPS = 1e-5
ROPE_THETA = 500000.0

ENGS = ("pe", "act", "dve", "pool", "sp")
PSUM_SERIAL = False
DBG_LOG = None
SAME_ENG_DIST = 10 ** 9
N_DMA_SEMS = 24


class Op:
    __slots__ = ("eng", "fn", "deps", "dma", "sig", "cnt", "dsem", "dval", "dprev")

    def __init__(self, eng, fn, deps, dma):
        self.eng = eng
        self.fn = fn
        self.deps = deps
        self.dma = dma
        self.sig = False
        self.cnt = 0
        self.dsem = None
        self.dval = 0
        self.dprev = 0


class Prog:
    def __init__(self, nc):
        self.nc = nc
        self.ops = []
        self.last_w = {}
        self.readers = {}

    def add(self, eng, fn, r=(), w=(), dma=False):
        idx = len(self.ops)
        if PSUM_SERIAL and eng in ("act", "dve") and any(isinstance(k, tuple) and k[0] == "ps" for k in r):
            w = list(w) + ["PSRD"]
        deps = set()
        for k in r:
            lw = self.last_w.get(k)
            if lw is not None:
                deps.add(lw)
        for k in w:
            lw = self.last_w.get(k)
            if lw is not None:
                deps.add(lw)
            deps.update(self.readers.get(k, ()))
        for k in r:
            self.readers.setdefault(k, []).append(idx)
        for k in w:
            self.last_w[k] = idx
            self.readers[k] = []
        deps.discard(idx)
        self.ops.append(Op(eng, fn, deps, dma))
        return idx

    def pe(self, fn, r=(), w=()):
        return self.add("pe", fn, r, w)

    def act(self, fn, r=(), w=()):
        return self.add("act", fn, r, w)

    def dve(self, fn, r=(), w=()):
        return self.add("dve", fn, r, w)

    def dma(self, q, fn, r=(), w=()):
        return self.add(q, fn, r, w, dma=True)

    def emit(self, final_wait_ops=()):
        nc = self.nc
        ops = self.ops
        pos = {}
        ctr = {e: 0 for e in ENGS}
        for i, op in enumerate(ops):
            pos[i] = ctr[op.eng]
            ctr[op.eng] += 1
        self.pos = pos

        def same_eng_skip(i, d):
            op, dop = ops[i], ops[d]
            if op.dma or dop.dma or op.eng != dop.eng:
                return False
            if op.eng == "pe":
                return True
            return op.eng in ("act", "dve") and pos[i] - pos[d] >= SAME_ENG_DIST

        self.same_eng_skip = same_eng_skip
        for i, op in enumerate(ops):
            for d in op.deps:
                dop = ops[d]
                if dop.dma:
                    continue
                if same_eng_skip(i, d):
                    continue
                dop.sig = True
        cnt = {e: 0 for e in ENGS}
        dma_uses = [0] * N_DMA_SEMS
        ndma = {"sp": 0, "pool": 0, "act": 0}
        for op in ops:
            if op.dma:
                if op.eng == "pool":
                    s = 8 + ndma["pool"] % (N_DMA_SEMS - 8)
                else:
                    s = ndma[op.eng] % 8
                ndma[op.eng] += 1
                op.dsem = s
                op.dprev = 16 * dma_uses[s]
                dma_uses[s] += 1
                op.dval = 16 * dma_uses[s]
            elif op.sig:
                cnt[op.eng] += 1
                op.cnt = cnt[op.eng]
        with contextlib.ExitStack() as st:
            esem = {e: st.enter_context(nc.semaphore("s_" + e)) for e in ENGS}
            dsem = [st.enter_context(nc.semaphore("d_%d" % i)) for i in range(N_DMA_SEMS)]
            block = st.enter_context(nc.Block())
            per_eng = {e: [] for e in ENGS}
            for i, op in enumerate(ops):
                per_eng[op.eng].append(i)

            def run_engine(ename, eng):
                waited = {}

                def wait(semkey, sem, val):
                    if waited.get(semkey, 0) >= val:
                        return
                    waited[semkey] = val
                    eng.wait_ge(sem, val)
                    if DBG_LOG is not None:
                        DBG_LOG.append((ename, "wait", semkey, val))

                for i in per_eng[ename]:
                    op = ops[i]
                    for d in sorted(op.deps):
                        dop = ops[d]
                        if dop.dma:
                            wait(("d", dop.dsem), dsem[dop.dsem], dop.dval)
                        else:
                            if self.same_eng_skip(i, d):
                                continue
                            wait(("e", dop.eng), esem[dop.eng], dop.cnt)
                    if op.dma:
                        if op.dprev > 0:
                            wait(("d", op.dsem), dsem[op.dsem], op.dprev)
                        ins = op.fn(eng)
                        ins.then_inc(dsem[op.dsem], 16)
                    else:
                        ins = op.fn(eng)
                        if op.sig:
                            ins.then_inc(esem[ename], 1)
                        if DBG_LOG is not None:
                            DBG_LOG.append((ename, "op", i, op.cnt if op.sig else None, str(ins)[:90]))
                if ename == "sp":
                    for i in final_wait_ops:
                        op = ops[i]
                        wait(("d", op.dsem), dsem[op.dsem], op.dval)

            @block.tensor
            def _(e):
                run_engine("pe", e)

            @block.scalar
            def _(e):
                run_engine("act", e)

            @block.vector
            def _(e):
                run_engine("dve", e)

            @block.gpsimd
            def _(e):
                run_engine("pool", e)

            @block.sync
            def _(e):
                run_engine("sp", e)


V_C = 0
V_ADAB = V_C + 8
V_LNG = V_ADAB + 192
V_LNB = V_LNG + 64
V_SUBG = V_LNB + 64
V_PASTNEG = V_SUBG + 2
V_NEGPAST = V_PASTNEG + 128
V_ONE = V_NEGPAST + 128
V_ZERO = V_ONE + 8
V_EPS = V_ZERO + 8
V_LAM = V_EPS + 1
NV = V_LAM + 512


STAGE = 99
SUB = 99
NPAIRS = 8
ROPE = 1
NODYN = 0
DBG_DEST = 0


def build(n_layers=DEPTH):
    nc = bass.Bass("TRN2", target_bir_lowering=False)

    def dr(name, shape, kind="ExternalInput"):
        return nc.dram_tensor(name, shape, F32, kind=kind).ap()

    xT_d = dr("xT", [128, 8, S])
    vec_d = dr("vec", [128, NV])
    cs_d = dr("cs", [128, 2, S])
    msk_d = dr("msk", [128, 8, 512])
    cst_d = dr("cst", [128, 3, 128])
    id4_d = dr("id4", [128, 512])
    nmo, ndi = (n_layers + 1) // 2, max(1, n_layers // 2)
    ada_w_d = dr("ada_w", [n_layers, D, 6 * D])
    mwin_d = dr("moba_w_in", [nmo, D, 3 * D])
    mwout_d = dr("moba_w_out", [nmo, D, D])
    dwin_d = dr("diff_w_in", [ndi, D, 3 * D])
    dwout_d = dr("diff_w_out", [ndi, D, D])
    wup_d = dr("mlp_w_up", [n_layers, D, DFF])
    wdn_d = dr("mlp_w_down", [n_layers, DFF, D])
    out_d = dr("outT", [128, 8, S], kind="ExternalOutput")

    st = contextlib.ExitStack()
    with st:
        def sb(name, shape, dt):
            return st.enter_context(nc.sbuf_tensor("sb_" + name, shape, dt))

        X = sb("X", [128, 8, S], F32)
        hT = sb("hT", [128, 8, S], BF16)
        CS = sb("CS", [128, 2, S], F32)
        MSK = sb("MSK", [128, 8, 512], BF16)
        CST = sb("CST", [128, 3, 128], BF16)
        vec = sb("vec", [128, NV], F32)
        mod = sb("mod", [128, DEPTH * 48], F32)
        dv = sb("dv", [128, 2 * DEPTH, 6, 8], F32)
        scb = sb("scb", [128, 8], BF16)
        lamv = sb("lamv", [128, 16], F32)
        lamt = sb("lamt", [128, 2, 64], F32)
        t1 = [sb("t1_0", [128, 512], F32)] * 2
        t2 = [sb("t2_0", [128, 512], F32)] * 2
        fin = [sb("fin_%d" % i, [128, 512], F32) for i in range(2)]
        lnx = [sb("lnx_%d" % i, [128, 512], BF16) for i in range(2)]
        lnq = [sb("lnq_%d" % i, [128, 512], BF16) for i in range(2)]
        gsm = sb("gsm", [128, 96], F32)
        gmA = sb("gmA", [128, 16, 8], F32)
        topA = sb("topA", [128, 16, 8], F32)
        notA = sb("notA", [128, 16, 8], F32)
        kmb = sb("kmb", [128, 128], BF16)
        ID4 = sb("ID4", [128, 512], BF16)
        bq3 = sb("bq3", [128, 8, 16], F32)
        ARENA = 29760
        arena = sb("arena", [128, ARENA], BF16)
        ps = [st.enter_context(nc.psum_tensor("ps%d" % i, [128, 512], F32)) for i in range(8)]

        off = [0]

        def carve(n):
            a = off[0]
            off[0] += n
            return a

        def view2(a, n):
            return arena[:, a:a + n]

        def view3(a, n0, n1):
            return arena[:, a:a + n0 * n1].rearrange("p (a b) -> p a b", a=n0)

        Wp = [view3(carve(3072), 8, 384) for _ in range(2)]
        Wout = [view2(carve(1024), 1024) for _ in range(2)]
        qT = [view2(carve(2048), 2048) for _ in range(2)]
        kT = [view2(carve(2048), 2048) for _ in range(2)]
        Vp = [view3(carve(2048), 16, 128) for _ in range(2)]
        oT = [view2(carve(2048), 2048) for _ in range(2)]
        qsb = [view2(carve(512), 512) for _ in range(2)]
        PT = [view2(carve(512), 512) for _ in range(4)]
        Dt = [view2(carve(512), 512) for _ in range(4)]
        assert off[0] <= ARENA
        off[0] = 0
        Wup = [view3(carve(4096), 8, 512) for _ in range(2)]
        Wdn = [view3(carve(4096), 4, 1024) for _ in range(2)]
        uT = [view3(carve(2048), 4, 512) for _ in range(2)]

        ident = CST[:, 0, :]
        ones = CST[:, 1, :]
        Pm = CST[:, 2, :]
        Ctab = CS[:, 0, :]
        Stab = CS[:, 1, :]

        P = Prog(nc)
        gb = {"i": 0, "n": 2}

        def gbank():
            b = gb["i"] % gb["n"]
            gb["i"] += 1
            return b

        def trs(tr):
            return slice(tr * 512, (tr + 1) * 512)

        def pk(bank):
            return [("ps", bank, 0), ("ps", bank, 1)]

        def barrier():
            P.dve(lambda e: e.memset(gsm[:, 90:91], 0.0), w=["arena"])

        for c in range(8):
            P.dma("sp", lambda e, c=c: e.dma_start(out=X[:, c, :], in_=xT_d[:, c, :]),
                  w=[("X", c, tr) for tr in range(4)])
        P.dma("sp", lambda e: e.dma_start(out=vec[:], in_=vec_d), w=["vec"])
        P.dma("sp", lambda e: e.dma_start(out=CS[:], in_=cs_d), w=["CS"])
        P.dma("pool", lambda e: e.dma_start(out=MSK[:], in_=msk_d), w=["MSK"])
        P.dma("pool", lambda e: e.dma_start(out=CST[:], in_=cst_d), w=["CST"])
        P.dma("pool", lambda e: e.dma_start(out=ID4[:], in_=id4_d), w=["ID4"])
        P.dve(lambda e: e.memset(kmb[:], 0.0), w=["kmb"])
        P.act(lambda e: e.activation(out=scb[:], in_=vec[:, V_C:V_C + 8], func=AF.Silu), r=["vec"], w=["scb"])

        def adaln(l):
            gb["n"] = 2
            bank = 7
            awv = ada_w_d[l].rearrange("(kc p) n -> p kc n", p=128)
            for n in range(12):
                b = n % 2
                P.dma("pool", lambda e, b=b, n=n: e.dma_start(out=Wup[b], in_=awv[:, :, n * 512:(n + 1) * 512]),
                      r=["arena"], w=[("wup", b)])
                for j in range(4):
                    col = n * 4 + j
                    for kc in range(8):
                        P.pe(lambda e, b=b, j=j, kc=kc, col=col: e.matmul(
                            ps[bank][:, col:col + 1], lhsT=Wup[b][:, kc, j * 128:(j + 1) * 128],
                            rhs=scb[:, kc:kc + 1], start=(kc == 0), stop=(kc == 7)),
                            r=[("wup", b), "scb", "arena"], w=pk(bank))
            P.dve(lambda e: e.tensor_tensor(out=mod[:, l * 48:(l + 1) * 48], in0=ps[bank][:, 0:48],
                                            in1=vec[:, V_ADAB + l * 48:V_ADAB + (l + 1) * 48], op=ALU.add),
                  r=pk(bank) + ["vec"], w=[("mod", l)])

        def derive(l, s):
            k = l * 2 + s
            base = l * 48 + s * 24
            shift = mod[:, base:base + 8]
            scale = mod[:, base + 8:base + 16]
            gate = mod[:, base + 16:base + 24]
            if k == 0:
                gp = vec[:, V_ONE:V_ONE + 8]
                bp = vec[:, V_ZERO:V_ZERO + 8]
            else:
                gp = vec[:, V_LNG + (k - 1) * 8:V_LNG + k * 8]
                bp = vec[:, V_LNB + (k - 1) * 8:V_LNB + k * 8]
            key = ("dv", k)
            rr = [("mod", l), "vec"]
            P.dve(lambda e: e.tensor_single_scalar(out=dv[:, k, 0, :], in_=scale, scalar=1.0, op=ALU.add), r=rr, w=[(key, 0)])
            P.dve(lambda e: e.tensor_tensor(out=dv[:, k, 1, :], in0=gp, in1=dv[:, k, 0, :], op=ALU.mult), r=rr + [(key, 0)], w=[(key, 1)])
            P.dve(lambda e: e.tensor_tensor(out=dv[:, k, 2, :], in0=bp, in1=dv[:, k, 0, :], op=ALU.mult), r=rr + [(key, 0)], w=[(key, 2)])
            P.dve(lambda e: e.tensor_tensor(out=dv[:, k, 2, :], in0=dv[:, k, 2, :], in1=shift, op=ALU.add), r=rr + [(key, 2)], w=[(key, 2)])
            P.dve(lambda e: e.tensor_single_scalar(out=dv[:, k, 3, :], in_=gp, scalar=ALPHA, op=ALU.mult), r=rr, w=[(key, 3)])
            P.dve(lambda e: e.tensor_single_scalar(out=dv[:, k, 4, :], in_=bp, scalar=ALPHA, op=ALU.mult), r=rr, w=[(key, 4)])
            P.dve(lambda e: e.tensor_single_scalar(out=dv[:, k, 5, :], in_=gate, scalar=1.0, op=ALU.add), r=rr, w=[(key, 5)])

        def make_h(k):
            key = ("dv", k)
            for c in range(8):
                for tr in range(4):
                    P.act(lambda e, c=c, tr=tr: e.activation(
                        out=hT[:, c, trs(tr)], in_=X[:, c, trs(tr)], func=AF.Identity,
                        bias=dv[:, k, 2, c:c + 1], scale=dv[:, k, 1, c:c + 1]),
                        r=[("X", c, tr), (key, 1), (key, 2)], w=[("h", c, tr)])
                    P.dve(lambda e, c=c, tr=tr: e.tensor_scalar(
                        out=X[:, c, trs(tr)], in0=X[:, c, trs(tr)], scalar1=dv[:, k, 3, c:c + 1],
                        scalar2=dv[:, k, 4, c:c + 1], op0=ALU.mult, op1=ALU.add),
                        r=[("X", c, tr), (key, 3), (key, 4)], w=[("X", c, tr)])

        def layer_norm():
            gb["n"] = 2
            for tr in range(4):
                bm, bq = 2 + (tr % 2) * 2, 3 + (tr % 2) * 2
                for c in range(8):
                    i = c % 2
                    P.act(lambda e, c=c, tr=tr, i=i: e.copy(out=lnx[i][:], in_=X[:, c, trs(tr)]),
                          r=[("X", c, tr)], w=[("lnx", i)])
                    P.dve(lambda e, c=c, tr=tr, i=i: e.tensor_tensor(out=lnq[i][:], in0=X[:, c, trs(tr)],
                                                                    in1=X[:, c, trs(tr)], op=ALU.mult),
                          r=[("X", c, tr)], w=[("lnq", i)])
                    P.pe(lambda e, c=c, i=i, bm=bm: e.matmul(ps[bm][:], lhsT=ones, rhs=lnx[i][:], start=(c == 0), stop=(c == 7)),
                         r=[("lnx", i), "CST"], w=pk(bm))
                    P.pe(lambda e, c=c, i=i, bq=bq: e.matmul(ps[bq][:], lhsT=ones, rhs=lnq[i][:], start=(c == 0), stop=(c == 7)),
                         r=[("lnq", i), "CST"], w=pk(bq))
                mu, rs = fin[0], fin[1]
                P.dve(lambda e, bm=bm: e.tensor_single_scalar(out=mu[:], in_=ps[bm][:], scalar=1.0 / D, op=ALU.mult),
                      r=pk(bm), w=[("fin", 0)])
                P.dve(lambda e: e.tensor_tensor(out=rs[:], in0=mu[:], in1=mu[:], op=ALU.mult), r=[("fin", 0)], w=[("fin", 1)])
                P.dve(lambda e, bq=bq: e.scalar_tensor_tensor(out=rs[:], in0=ps[bq][:], scalar=1.0 / D, in1=rs[:],
                                                             op0=ALU.mult, op1=ALU.subtract),
                      r=pk(bq) + [("fin", 1)], w=[("fin", 1)])
                P.act(lambda e: e.activation(out=rs[:], in_=rs[:], func=AF.Ln, bias=vec[:, V_EPS:V_EPS + 1], scale=1.0),
                      r=[("fin", 1), "vec"], w=[("fin", 1)])
                P.act(lambda e: e.activation(out=rs[:], in_=rs[:], func=AF.Exp, scale=-0.5), r=[("fin", 1)], w=[("fin", 1)])
                for c in range(8):
                    P.dve(lambda e, c=c, tr=tr: e.tensor_tensor(out=X[:, c, trs(tr)], in0=X[:, c, trs(tr)], in1=mu[:],
                                                               op=ALU.subtract),
                          r=[("X", c, tr), ("fin", 0)], w=[("X", c, tr)])
                    P.dve(lambda e, c=c, tr=tr: e.tensor_tensor(out=X[:, c, trs(tr)], in0=X[:, c, trs(tr)], in1=rs[:],
                                                               op=ALU.mult),
                          r=[("X", c, tr), ("fin", 1)], w=[("X", c, tr)])

        def attention(l, kind):
            k = l * 2
            j = l // 2
            gb["n"] = 2
            win_d = (mwin_d if kind == "moba" else dwin_d)[j].rearrange("(kc p) n -> p kc n", p=128)
            wout_d = (mwout_d if kind == "moba" else dwout_d)[j]
            GT = dv[:, k, 5, :]
            if kind == "diff":
                lam_init = 0.8 - 0.6 * math.exp(-0.3 * l)
                lb = V_LAM + j * 256
                for i2 in range(2):
                    P.dve(lambda e, i2=i2: e.tensor_tensor(out=lamt[:, i2, :], in0=vec[:, lb + i2 * 128:lb + i2 * 128 + 64],
                                                          in1=vec[:, lb + i2 * 128 + 64:lb + i2 * 128 + 128], op=ALU.mult),
                          r=["vec"], w=[("lamt", i2)])
                P.dve(lambda e: e.tensor_reduce(out=lamv[:, 2:4], in_=lamt[:], axis=AX.X, op=ALU.add),
                      r=[("lamt", 0), ("lamt", 1)], w=["lamv23"])
                P.act(lambda e: e.activation(out=lamv[:, 4:6], in_=lamv[:, 2:4], func=AF.Exp), r=["lamv23"], w=["lamv45"])
                P.dve(lambda e: e.scalar_tensor_tensor(out=lamv[:, 0:1], in0=lamv[:, 5:6], scalar=-lam_init, in1=lamv[:, 4:5],
                                                       op0=ALU.add, op1=ALU.subtract),
                      r=["lamv45"], w=["neglam"])
                P.dve(lambda e: e.tensor_single_scalar(out=lamv[:, 1:2], in_=vec[:, V_SUBG + j:V_SUBG + j + 1],
                                                       scalar=(1.0 - lam_init), op=ALU.mult),
                      r=["vec"], w=["gs"])

            pending = []

            def flush(force=False):
                keep = []
                for item in pending:
                    item[0] -= 1
                    if force or item[0] <= 0:
                        item[1]()
                    else:
                        keep.append(item)
                pending[:] = keep

            def do_pair(p):
                b = p % 2
                for which in range(3):
                    P.dma("pool", lambda e, b=b, which=which, p=p: e.dma_start(
                        out=Wp[b][:, :, which * 128:(which + 1) * 128],
                        in_=win_d[:, :, which * 1024 + p * 128:which * 1024 + (p + 1) * 128]),
                        r=["arena"], w=[("wp", b, which)])
                P.dma("pool", lambda e, b=b, p=p: e.dma_start(out=Wout[b], in_=wout_d[p * 128:(p + 1) * 128, :]),
                      r=["arena"], w=[("wout", b)])
                if SUB < 1:
                    return
                for which, dst, dkey in ((0, qT[b], "qT"), (1, kT[b], "kT")):
                    for tr in range(4):
                        bank = gbank()
                        for kc in range(8):
                            P.pe(lambda e, b=b, which=which, tr=tr, kc=kc, bank=bank: e.matmul(
                                ps[bank][:], lhsT=Wp[b][:, kc, which * 128:(which + 1) * 128], rhs=hT[:, kc, trs(tr)],
                                start=(kc == 0), stop=(kc == 7)),
                                r=[("wp", b, which), ("h", kc, tr), "arena"], w=pk(bank))
                        i = tr % 2
                        P.act(lambda e, i=i, bank=bank: e.copy(out=qsb[i], in_=ps[bank][:]),
                              r=pk(bank) + ["arena"], w=[("qsb", i)])
                        if not ROPE:
                            P.act(lambda e, bank=bank, dst=dst, tr=tr: e.copy(out=dst[:, trs(tr)], in_=ps[bank][:]),
                                  r=pk(bank) + ["arena"], w=[(dkey, b, tr)])
                            continue
                        bank2 = gbank()
                        P.pe(lambda e, i=i, bank2=bank2: e.matmul(ps[bank2][:], lhsT=Pm, rhs=qsb[i], start=True, stop=True),
                             r=[("qsb", i), "CST"], w=pk(bank2))
                        P.dve(lambda e, i=i, bank=bank, tr=tr: e.tensor_tensor(out=t1[i][:], in0=ps[bank][:], in1=Ctab[:, trs(tr)],
                                                                              op=ALU.mult),
                              r=pk(bank) + ["CS", ("qsb", i)], w=[("t1", 0)])
                        P.dve(lambda e, i=i, bank2=bank2, tr=tr: e.tensor_tensor(out=t2[i][:], in0=ps[bank2][:],
                                                                                in1=Stab[:, trs(tr)], op=ALU.mult),
                              r=pk(bank2) + ["CS"], w=[("t2", 0)])
                        P.dve(lambda e, i=i, dst=dst, tr=tr: e.tensor_tensor(out=dst[:, trs(tr)], in0=t1[i][:], in1=t2[i][:],
                                                                            op=ALU.add),
                              r=[("t1", 0), ("t2", 0), "arena"], w=[(dkey, b, tr)])
                if SUB < 2:
                    return
                for g4 in range(4):
                    bank = gbank()
                    for tt4 in range(4):
                        tt = g4 * 4 + tt4
                        for kc in range(8):
                            P.pe(lambda e, b=b, tt=tt, tt4=tt4, kc=kc, bank=bank: e.matmul(
                                ps[bank][:, tt4 * 128:(tt4 + 1) * 128], lhsT=hT[:, kc, tt * 128:(tt + 1) * 128],
                                rhs=Wp[b][:, kc, 256:384], start=(kc == 0), stop=(kc == 7)),
                                r=[("wp", b, 2), ("h", kc, g4), "arena"], w=pk(bank))
                    for tt4 in range(4):
                        P.act(lambda e, b=b, g4=g4, tt4=tt4, bank=bank: e.copy(
                            out=Vp[b][:, g4 * 4 + tt4, :], in_=ps[bank][:, tt4 * 128:(tt4 + 1) * 128]),
                            r=pk(bank) + ["arena"], w=[("V", b, g4, tt4)])
                flush(force=True)
                if SUB < 3:
                    return
                if kind == "moba" and not NODYN:
                    P.dve(lambda e, b=b: e.tensor_reduce(out=gsm[:, 48:56], in_=kT[b].rearrange("p (a b) -> p a b", a=8),
                                                         axis=AX.X, op=ALU.add),
                          r=[("kT", b, tr) for tr in range(4)], w=["kms"])
                    P.dve(lambda e: e.tensor_single_scalar(out=kmb[0:64, 0:8], in_=gsm[0:64, 48:56], scalar=1.0 / 256, op=ALU.mult),
                          r=["kms"], w=["kmb"])
                    P.dve(lambda e: e.tensor_single_scalar(out=kmb[64:128, 8:16], in_=gsm[64:128, 48:56], scalar=1.0 / 256,
                                                           op=ALU.mult),
                          r=["kms", "kmb"], w=["kmb"])
                    for qt in range(8):
                        gbk = 4 + qt // 4
                        P.pe(lambda e, b=b, qt=qt, gbk=gbk: e.matmul(
                            ps[gbk][:, (qt % 4) * 128:(qt % 4 + 1) * 128],
                            lhsT=qT[b][:, (8 + qt) * 128:(9 + qt) * 128], rhs=kmb[:], start=True, stop=True),
                            r=[("qT", b, (8 + qt) // 4), "kmb", "arena"], w=pk(gbk))
                    for g in range(16):
                        gbk = 4 + (g // 2) // 4
                        P.dve(lambda e, g=g, gbk=gbk: e.tensor_tensor(
                            out=gmA[:, g, :], in0=ps[gbk][:, ((g // 2) % 4) * 128 + (g % 2) * 8:((g // 2) % 4) * 128 + (g % 2) * 8 + 8],
                            in1=vec[:, V_PASTNEG + g * 8:V_PASTNEG + g * 8 + 8], op=ALU.add),
                            r=pk(gbk) + ["vec"], w=[("gm", g)])
                        P.dve(lambda e, g=g: e.max(out=topA[:, g, :], in_=gmA[:, g, :]), r=[("gm", g)], w=[("top8", g)])
                        P.dve(lambda e, g=g: e.tensor_scalar(out=notA[:, g, :], in0=gmA[:, g, :], scalar1=topA[:, g, 2:3],
                                                             scalar2=None, op0=ALU.is_lt),
                              r=[("gm", g), ("top8", g)], w=[("nots", g)])
                    P.dve(lambda e: e.tensor_tensor(out=bq3[:].rearrange("p a b -> p (a b)"),
                                                    in0=notA[:].rearrange("p a b -> p (a b)"),
                                                    in1=vec[:, V_NEGPAST:V_NEGPAST + 128], op=ALU.mult),
                          r=[("nots", g) for g in range(16)] + ["vec"], w=[("bq3", qt) for qt in range(8)])
                if SUB == 3.5:
                    bankd = 6
                    P.pe(lambda e, bankd=bankd: e.matmul(ps[bankd][:, 0:128], lhsT=ident, rhs=ones, start=True, stop=True),
                         r=["CST"] + ([("bq3", 0)] if DBG_DEST == 0 else []), w=pk(bankd))
                    P.dve(lambda e: e.memset(gsm[:, 91:92], 0.0), r=pk(bankd), w=[("X", c, tr) for c in range(8) for tr in range(4)])
                if SUB < 4:
                    return

                def emit_pv(qr, kt, hh, pti, first, last):
                    if kind == "moba":
                        ob, sbk = (4, 5) if qr % 2 == 0 else (6, 7)
                        rows = slice(hh * 64, (hh + 1) * 64)
                        P.pe(lambda e: e.matmul(ps[ob][rows, :], lhsT=Vp[b][:, kt, hh * 64:(hh + 1) * 64], rhs=PT[pti],
                                                start=first, stop=last),
                             r=[("V", b, kt // 4, kt % 4), ("PT", pti), "arena"], w=[("ps", ob, hh)])
                        P.pe(lambda e: e.matmul(ps[sbk][rows, :], lhsT=ones[:, 0:64], rhs=PT[pti], start=first, stop=last),
                             r=[("PT", pti), "CST"], w=[("ps", sbk, hh)])
                    else:
                        ob, sbk = 4 + 2 * hh, 5 + 2 * hh
                        P.pe(lambda e: e.matmul(ps[ob][:], lhsT=Vp[b][:, kt, :], rhs=PT[pti], start=first, stop=last),
                             r=[("V", b, kt // 4, kt % 4), ("PT", pti), "arena"], w=pk(ob))
                        P.pe(lambda e: e.matmul(ps[sbk][:], lhsT=ones, rhs=PT[pti], start=first, stop=last),
                             r=[("PT", pti), "CST"], w=pk(sbk))

                def finalize(qr):
                    tr = qr
                    if kind == "moba":
                        ob, sbk = (4, 5) if qr % 2 == 0 else (6, 7)
                        f = fin[qr % 2]
                        P.dve(lambda e: e.reciprocal(out=f[:], in_=ps[sbk][:]),
                              r=pk(sbk), w=[("fin", qr % 2)])
                        P.dve(lambda e: e.tensor_tensor(out=oT[b][:, trs(tr)], in0=ps[ob][:], in1=f[:], op=ALU.mult),
                              r=pk(ob) + [("fin", qr % 2), "arena"], w=[("oT", b, tr)])
                    else:
                        f0, f1 = fin[0], fin[1]
                        P.dve(lambda e: e.reciprocal(out=f0[:], in_=ps[5][:]), r=pk(5), w=[("fin", 0)])
                        P.dve(lambda e: e.tensor_tensor(out=f0[:], in0=ps[4][:], in1=f0[:], op=ALU.mult),
                              r=pk(4) + [("fin", 0)], w=[("fin", 0)])
                        P.dve(lambda e: e.reciprocal(out=f1[:], in_=ps[7][:]), r=pk(7), w=[("fin", 1)])
                        P.dve(lambda e: e.tensor_tensor(out=f1[:], in0=ps[6][:], in1=f1[:], op=ALU.mult),
                              r=pk(6) + [("fin", 1)], w=[("fin", 1)])
                        P.dve(lambda e: e.scalar_tensor_tensor(out=f0[:], in0=f1[:], scalar=lamv[:, 0:1], in1=f0[:],
                                                               op0=ALU.mult, op1=ALU.add),
                              r=[("fin", 0), ("fin", 1), "neglam"], w=[("fin", 0)])
                        P.dve(lambda e: e.tensor_tensor(out=lnq[0][:], in0=f0[:], in1=f0[:], op=ALU.mult),
                              r=[("fin", 0)], w=[("lnq", 0)])
                        bank = gbank()
                        P.pe(lambda e: e.matmul(ps[bank][:], lhsT=ones, rhs=lnq[0][:], start=True, stop=True),
                             r=[("lnq", 0), "CST"], w=pk(bank))
                        P.act(lambda e: e.activation(out=f1[:], in_=ps[bank][:], func=AF.Ln, bias=vec[:, V_EPS:V_EPS + 1],
                                                     scale=1.0 / 128),
                              r=pk(bank) + ["vec", ("fin", 1)], w=[("fin", 1)])
                        P.act(lambda e: e.activation(out=f1[:], in_=f1[:], func=AF.Exp, scale=-0.5),
                              r=[("fin", 1)], w=[("fin", 1)])
                        P.dve(lambda e: e.scalar_tensor_tensor(out=oT[b][:, trs(tr)], in0=f0[:], scalar=lamv[:, 1:2], in1=f1[:],
                                                               op0=ALU.mult, op1=ALU.mult),
                              r=[("fin", 0), ("fin", 1), "gs", "arena"], w=[("oT", b, tr)])

                def outproj(qr):
                    tr = qr
                    if SUB < 5:
                        return
                    for oc in range(8):
                        bank = gbank()
                        P.pe(lambda e, oc=oc, bank=bank: e.matmul(ps[bank][:], lhsT=Wout[b][:, oc * 128:(oc + 1) * 128],
                                                                  rhs=oT[b][:, trs(tr)], start=True, stop=True),
                             r=[("wout", b), ("oT", b, tr), "arena"], w=pk(bank))
                        P.dve(lambda e, oc=oc, bank=bank: e.scalar_tensor_tensor(
                            out=X[:, oc, trs(tr)], in0=ps[bank][:], scalar=GT[:, oc:oc + 1], in1=X[:, oc, trs(tr)],
                            op0=ALU.mult, op1=ALU.add),
                            r=pk(bank) + [("X", oc, tr), (("dv", k), 5)], w=[("X", oc, tr)])

                prev = []
                pti_ctr = [0]
                for qr in range(4):
                    nkt = 4 * qr + 4
                    for kt in range(nkt):
                        cur = []
                        for hh in range(2):
                            sbank = 2 + hh
                            pti = pti_ctr[0] % 4
                            pti_ctr[0] += 1
                            jd = kt - 4 * qr
                            extra = []
                            if jd >= 0:
                                extra.append(("static", (0 if kind == "moba" else 4) + jd))
                            if kind == "moba" and not NODYN and qr >= 2 and (kt // 2) < 2 * qr + 1:
                                nblk = kt // 2
                                slot = hh * 2 + (nblk % 2)
                                if kt % 2 == 0:
                                    col = hh * 8 + nblk
                                    for q4 in range(4):
                                        P.dve(lambda e, slot=slot, col=col, qr=qr, q4=q4: e.tensor_scalar(
                                            out=Dt[slot][:, q4 * 128:(q4 + 1) * 128], in0=ident,
                                            scalar1=bq3[:, (qr - 2) * 4 + q4, col:col + 1], scalar2=None, op0=ALU.mult),
                                            r=[("bq3", (qr - 2) * 4 + q4), "CST", "arena"], w=[("D", slot, q4)])
                                extra.append(("dyn", slot))
                            nmm = 1 + len(extra)
                            P.pe(lambda e, hh=hh, kt=kt, qr=qr, sbank=sbank, nmm=nmm: e.matmul(
                                ps[sbank][:], lhsT=kT[b][hh * 64:(hh + 1) * 64, kt * 128:(kt + 1) * 128],
                                rhs=qT[b][hh * 64:(hh + 1) * 64, trs(qr)], start=True, stop=(nmm == 1)),
                                r=[("kT", b, kt // 4), ("qT", b, qr), "arena"], w=pk(sbank))
                            for xi, (xk, xv) in enumerate(extra):
                                lastx = (xi == len(extra) - 1)
                                if xk == "static":
                                    P.pe(lambda e, xv=xv, sbank=sbank, lastx=lastx: e.matmul(
                                        ps[sbank][:], lhsT=ident, rhs=MSK[:, xv, :], start=False, stop=lastx),
                                        r=["MSK", "CST"], w=pk(sbank))
                                else:
                                    P.pe(lambda e, xv=xv, sbank=sbank, lastx=lastx: e.matmul(
                                        ps[sbank][:], lhsT=ones, rhs=Dt[xv], start=False, stop=lastx),
                                        r=[("D", xv, q4) for q4 in range(4)] + ["CST", "arena"], w=pk(sbank))
                            P.act(lambda e, sbank=sbank, pti=pti: e.activation(out=PT[pti], in_=ps[sbank][:], func=AF.Exp,
                                                                                scale=0.125),
                                  r=pk(sbank) + ["arena"], w=[("PT", pti)])
                            cur.append((qr, kt, hh, pti, kt == 0, kt == nkt - 1))
                        for a in prev:
                            emit_pv(*a)
                        if prev and prev[0][5]:
                            pqr = prev[0][0]
                            finalize(pqr)
                            pending.append([2, (lambda pqr=pqr: outproj(pqr))])
                        prev = cur
                        flush()
                for a in prev:
                    emit_pv(*a)
                finalize(3)
                pending.append([1, (lambda: outproj(3))])

            for p in range(NPAIRS):
                do_pair(p)
            flush(force=True)

        def mlp(l):
            k = l * 2 + 1
            GT = dv[:, k, 5, :]
            gb["n"] = 2
            upv = wup_d[l].rearrange("(kc p) n -> p kc n", p=128)
            dnv = wdn_d[l].rearrange("(fc p) n -> p fc n", p=128)
            ubank = [0]
            ybank = [0]

            def up(g, tr, ui):
                b = g % 2
                for fc in range(4):
                    bank = ubank[0] % 4
                    ubank[0] += 1
                    for kc in range(8):
                        P.pe(lambda e, fc=fc, kc=kc, bank=bank: e.matmul(
                            ps[bank][:], lhsT=Wup[b][:, kc, fc * 128:(fc + 1) * 128], rhs=hT[:, kc, trs(tr)],
                            start=(kc == 0), stop=(kc == 7)),
                            r=[("wup", b), ("h", kc, tr), "arena"], w=pk(bank))
                    ti = fc % 2
                    P.act(lambda e, bank=bank, ti=ti: e.activation(out=t1[ti][:], in_=ps[bank][:], func=AF.Relu),
                          r=pk(bank), w=[("t1", 0)])
                    P.dve(lambda e, fc=fc, ti=ti: e.tensor_tensor(out=uT[ui][:, fc, :], in0=t1[ti][:], in1=t1[ti][:], op=ALU.mult),
                          r=[("t1", 0), "arena"], w=[("uT", ui, fc)])

            def down(g, tr, ui):
                b = g % 2
                for oc in range(8):
                    bank = 4 + ybank[0] % 4
                    ybank[0] += 1
                    for fc in range(4):
                        P.pe(lambda e, fc=fc, oc=oc, bank=bank: e.matmul(
                            ps[bank][:], lhsT=Wdn[b][:, fc, oc * 128:(oc + 1) * 128], rhs=uT[ui][:, fc, :],
                            start=(fc == 0), stop=(fc == 3)),
                            r=[("wdn", b), ("uT", ui, fc), "arena"], w=pk(bank))
                    P.dve(lambda e, oc=oc, bank=bank: e.scalar_tensor_tensor(
                        out=X[:, oc, trs(tr)], in0=ps[bank][:], scalar=GT[:, oc:oc + 1], in1=X[:, oc, trs(tr)],
                        op0=ALU.mult, op1=ALU.add),
                        r=pk(bank) + [("X", oc, tr), (("dv", k), 5)], w=[("X", oc, tr)])

            prev = None
            step = 0
            for g in range(8):
                b = g % 2
                P.dma("pool", lambda e, b=b, g=g: e.dma_start(out=Wup[b], in_=upv[:, :, g * 512:(g + 1) * 512]),
                      r=["arena"], w=[("wup", b)])
                P.dma("pool", lambda e, b=b, g=g: e.dma_start(out=Wdn[b], in_=dnv[:, g * 4:(g + 1) * 4, :]),
                      r=["arena"], w=[("wdn", b)])
                for tr in range(4):
                    ui = step % 2
                    step += 1
                    up(g, tr, ui)
                    if prev is not None:
                        down(*prev)
                    prev = (g, tr, ui)
            down(*prev)

        adaln(0)
        for l in range(n_layers):
            kind = "moba" if l % 2 == 0 else "diff"
            derive(l, 0)
            derive(l, 1)
            if STAGE < 2:
                break
            barrier()
            make_h(2 * l)
            if STAGE < 3:
                break
            attention(l, kind)
            if STAGE < 4:
                break
            layer_norm()
            if STAGE < 5:
                break
            barrier()
            if l + 1 < n_layers:
                adaln(l + 1)
            make_h(2 * l + 1)
            if STAGE < 6:
                break
            mlp(l)
            if STAGE < 7:
                break
            layer_norm()
        kl = 2 * n_layers - 1
        outs = []
        for c in range(8):
            for tr in range(4):
                P.dve(lambda e, c=c, tr=tr: e.tensor_scalar(
                    out=X[:, c, trs(tr)], in0=X[:, c, trs(tr)], scalar1=vec[:, V_LNG + kl * 8 + c:V_LNG + kl * 8 + c + 1],
                    scalar2=vec[:, V_LNB + kl * 8 + c:V_LNB + kl * 8 + c + 1], op0=ALU.mult, op1=ALU.add),
                    r=[("X", c, tr), "vec"], w=[("X", c, tr)])
            outs.append(P.dma("sp", lambda e, c=c: e.dma_start(out=out_d[:, c, :], in_=X[:, c, :]),
                              r=[("X", c, tr) for tr in range(4)]))
        P.emit(final_wait_ops=outs)
    return nc


def _consts():
    pos = np.arange(S, dtype=np.float32)
    inv = (np.float32(ROPE_THETA) ** (-np.arange(0, 16, 2, dtype=np.float32) / np.float32(16))).astype(np.float32)
    ang = (pos[:, None] * inv[None, :]).astype(np.float32)
    cos = np.cos(ang).astype(np.float32)
    sin = np.sin(ang).astype(np.float32)
    cs = np.zeros((128, 2, S), np.float32)
    cs[:, 0, :] = 1.0
    for hh in range(2):
        for d in range(8):
            cs[hh * 64 + d, 0, :] = cos[:, d]
            cs[hh * 64 + 8 + d, 0, :] = cos[:, d]
            cs[hh * 64 + d, 1, :] = -sin[:, d]
            cs[hh * 64 + 8 + d, 1, :] = sin[:, d]
    msk = np.zeros((128, 8, 512), np.float32)
    kk = np.arange(128)[:, None]
    qq = np.arange(512)[None, :]
    for j in range(4):
        kp = j * 128 + kk
        same0 = (kp < 256) & (qq < 256) & (kp <= qq)
        past = (kp < 256) & (qq >= 256)
        same1 = (kp >= 256) & (qq >= 256) & (kp <= qq)
        msk[:, j, :] = np.where(same0 | past | same1, 0.0, NEG)
        msk[:, 4 + j, :] = np.where(kp <= qq, 0.0, NEG)
    cst = np.zeros((128, 3, 128), np.float32)
    cst[:, 0, :] = np.eye(128, dtype=np.float32)
    cst[:, 1, :] = 1.0
    for hh in range(2):
        for d in range(8):
            cst[hh * 64 + d + 8, 2, hh * 64 + d] = 1.0
            cst[hh * 64 + d, 2, hh * 64 + d + 8] = 1.0
    return cs, msk, cst


def _vec(c_b, ada_b, ln_g, ln_b, subln_g, lams):
    v = np.zeros((128, NV), np.float32)
    v[:, V_C:V_C + 8] = c_b.reshape(8, 128).T
    v[:, V_ADAB:V_ADAB + 192] = ada_b.reshape(DEPTH, 48, 128).transpose(2, 0, 1).reshape(128, 192)
    v[:, V_LNG:V_LNG + 64] = ln_g.reshape(DEPTH * 2, 8, 128).transpose(2, 0, 1).reshape(128, 64)
    v[:, V_LNB:V_LNB + 64] = ln_b.reshape(DEPTH * 2, 8, 128).transpose(2, 0, 1).reshape(128, 64)
    v[:, V_SUBG:V_SUBG + 2] = subln_g.T
    for qt in range(8):
        qb = 4 + qt // 2
        for hh in range(2):
            for n in range(8):
                v[:, V_PASTNEG + qt * 16 + hh * 8 + n] = 0.0 if n < qb else -1e30
                v[:, V_NEGPAST + qt * 16 + hh * 8 + n] = NEG if n < qb else 0.0
    v[:, V_ONE:V_ONE + 8] = 1.0
    v[:, V_EPS] = LN_EPS
    for j in range(2):
        for i, a in enumerate(lams):
            v[:, V_LAM + j * 256 + i * 64:V_LAM + j * 256 + (i + 1) * 64] = a[j][None, :]
    return v


_NC_CACHE = {}
STAGE = 99


def _run(inputs, n_layers, cores):
    x = np.asarray(inputs["x"], np.float32)
    c = np.asarray(inputs["c"], np.float32)
    cs, msk, cst = _consts()
    f = lambda k: np.ascontiguousarray(np.asarray(inputs[k], np.float32))
    nmo, ndi = (n_layers + 1) // 2, max(1, n_layers // 2)
    nsl = {"ada_w": n_layers, "mlp_w_up": n_layers, "mlp_w_down": n_layers, "moba_w_in": nmo, "moba_w_out": nmo,
           "diff_w_in": ndi, "diff_w_out": ndi}
    shared = {k: np.ascontiguousarray(np.asarray(inputs[k], np.float32)[:n]) for k, n in nsl.items()}
    lams = [f("diff_lam_q1"), f("diff_lam_k1"), f("diff_lam_q2"), f("diff_lam_k2")]
    in_maps = []
    for b in cores:
        xT = np.ascontiguousarray(x[b].T.reshape(8, 128, S).transpose(1, 0, 2))
        m = {"xT": xT, "vec": _vec(c[b], f("ada_b"), f("ln_g"), f("ln_b"), f("diff_subln_g"), lams),
             "cs": cs, "msk": msk, "cst": cst, "id4": np.tile(np.eye(128, dtype=np.float32), (1, 4))}
        m.update(shared)
        in_maps.append(m)
    if n_layers not in _NC_CACHE:
        _NC_CACHE[n_layers] = build(n_layers)
    nc = _NC_CACHE[n_layers]
    res = run_bass_kernel_spmd(nc, in_maps, core_ids=list(range(len(cores))))
    outs = []
    for r in res.results:
        oT = r["outT"]
        outs.append(oT.transpose(2, 1, 0).reshape(S, D))
    return np.stack(outs, 0).astype(np.float32)


def kernel(**inputs):
    return _run(inputs, DEPTH, list(range(8)))
```

```python
import contextlib
import math

import numpy as np
import concourse.bass as bass
import concourse.mybir as mybir
from concourse.bass_utils import run_bass_kernel_spmd

F32 = mybir.dt.float32
BF16 = mybir.dt.bfloat16
ALU = mybir.AluOpType
AF = mybir.ActivationFunctionType
AX = mybir.AxisListType

D = 1024
S = 2048
DEPTH = 4
DFF = 4096
ALPHA = (2.0 * DEPTH) ** 0.25
NEG = -30000.0
LN_EPS = 1e-5
ROPE_THETA = 500000.0

ENGS = ("pe", "act", "dve", "pool", "sp")
PSUM_SERIAL = False
DBG_LOG = None
SAME_ENG_DIST = 10 ** 9
N_DMA_SEMS = 24


class Op:
    __slots__ = ("eng", "fn", "deps", "dma", "sig", "cnt", "dsem", "dval", "dprev")

    def __init__(self, eng, fn, deps, dma):
        self.eng = eng
        self.fn = fn
        self.deps = deps
        self.dma = dma
        self.sig = False
        self.cnt = 0
        self.dsem = None
        self.dval = 0
        self.dprev = 0


class Prog:
    def __init__(self, nc):
        self.nc = nc
        self.ops = []
        self.last_w = {}
        self.readers = {}

    def add(self, eng, fn, r=(), w=(), dma=False, r2=()):
        idx = len(self.ops)
        if PSUM_SERIAL and eng in ("act", "dve") and any(isinstance(k, tuple) and k[0] == "ps" for k in r):
            w = list(w) + ["PSRD"]
        deps = set()
        for k in list(r) + list(r2):
            lw = self.last_w.get(k)
            if lw is not None:
                deps.add(lw)
        for k in w:
            lw = self.last_w.get(k)
            if lw is not None:
                deps.add(lw)
            deps.update(self.readers.get(k, ()))
        for k in r:
            self.readers.setdefault(k, []).append(idx)
        for k in w:
            self.last_w[k] = idx
            self.readers[k] = []
        deps.discard(idx)
        self.ops.append(Op(eng, fn, deps, dma))
        return idx

    def pe(self, fn, r=(), w=(), r2=()):
        return self.add("pe", fn, r, w, r2=r2)

    def act(self, fn, r=(), w=()):
        return self.add("act", fn, r, w)

    def dve(self, fn, r=(), w=()):
        return self.add("dve", fn, r, w)

    def dma(self, q, fn, r=(), w=()):
        return self.add(q, fn, r, w, dma=True)

    def emit(self, final_wait_ops=()):
        nc = self.nc
        ops = self.ops
        pos = {}
        ctr = {e: 0 for e in ENGS}
        for i, op in enumerate(ops):
            pos[i] = ctr[op.eng]
            ctr[op.eng] += 1
        self.pos = pos

        def same_eng_skip(i, d):
            op, dop = ops[i], ops[d]
            if op.dma or dop.dma or op.eng != dop.eng:
                return False
            if op.eng == "pe":
                return True
            return op.eng in ("act", "dve") and pos[i] - pos[d] >= SAME_ENG_DIST

        self.same_eng_skip = same_eng_skip
        for i, op in enumerate(ops):
            for d in op.deps:
                dop = ops[d]
                if dop.dma:
                    continue
                if same_eng_skip(i, d):
                    continue
                dop.sig = True
        cnt = {e: 0 for e in ENGS}
        dma_uses = [0] * N_DMA_SEMS
        ndma = {"sp": 0, "pool": 0, "act": 0}
        for op in ops:
            if op.dma:
                if op.eng == "pool":
                    s = 8 + ndma["pool"] % (N_DMA_SEMS - 8)
                else:
                    s = ndma[op.eng] % 8
                ndma[op.eng] += 1
                op.dsem = s
                op.dprev = 16 * dma_uses[s]
                dma_uses[s] += 1
                op.dval = 16 * dma_uses[s]
            elif op.sig:
                cnt[op.eng] += 1
                op.cnt = cnt[op.eng]
        with contextlib.ExitStack() as st:
            esem = {e: st.enter_context(nc.semaphore("s_" + e)) for e in ENGS}
            dsem = [st.enter_context(nc.semaphore("d_%d" % i)) for i in range(N_DMA_SEMS)]
            block = st.enter_context(nc.Block())
            per_eng = {e: [] for e in ENGS}
            for i, op in enumerate(ops):
                per_eng[op.eng].append(i)

            def run_engine(ename, eng):
                waited = {}

                def wait(semkey, sem, val):
                    if waited.get(semkey, 0) >= val:
                        return
                    waited[semkey] = val
                    eng.wait_ge(sem, val)
                    if DBG_LOG is not None:
                        DBG_LOG.append((ename, "wait", semkey, val))

                for i in per_eng[ename]:
                    op = ops[i]
                    for d in sorted(op.deps):
                        dop = ops[d]
                        if dop.dma:
                            wait(("d", dop.dsem), dsem[dop.dsem], dop.dval)
                        else:
                            if self.same_eng_skip(i, d):
                                continue
                            wait(("e", dop.eng), esem[dop.eng], dop.cnt)
                    if op.dma:
                        if op.dprev > 0:
                            wait(("d", op.dsem), dsem[op.dsem], op.dprev)
                        ins = op.fn(eng)
                        ins.then_inc(dsem[op.dsem], 16)
                    else:
                        ins = op.fn(eng)
                        if op.sig:
                            ins.then_inc(esem[ename], 1)
                        if DBG_LOG is not None:
                            DBG_LOG.append((ename, "op", i, op.cnt if op.sig else None, str(ins)[:90]))
                if ename == "sp":
                    for i in final_wait_ops:
                        op = ops[i]
                        wait(("d", op.dsem), dsem[op.dsem], op.dval)

            @block.tensor
            def _(e):
                run_engine("pe", e)

            @block.scalar
            def _(e):
                run_engine("act", e)

            @block.vector
            def _(e):
                run_engine("dve", e)

            @block.gpsimd
            def _(e):
                run_engine("pool", e)

            @block.sync
            def _(e):
                run_engine("sp", e)


V_C = 0
V_ADAB = V_C + 8
V_LNG = V_ADAB + 192
V_LNB = V_LNG + 64
V_SUBG = V_LNB + 64
V_PASTNEG = V_SUBG + 2
V_NEGPAST = V_PASTNEG + 128
V_ONE = V_NEGPAST + 128
V_ZERO = V_ONE + 8
V_EPS = V_ZERO + 8
V_LAM = V_EPS + 1
NV = V_LAM + 512


STAGE = 99
SUB = 99
NPAIRS = 8
ROPE = 1
NODYN = 0
DBG_DEST = 0


def build(n_layers=DEPTH):
    nc = bass.Bass("TRN2", target_bir_lowering=False)

    def dr(name, shape, kind="ExternalInput"):
        return nc.dram_tensor(name, shape, F32, kind=kind).ap()

    xT_d = dr("xT", [128, 8, S])
    vec_d = dr("vec", [128, NV])
    cs_d = dr("cs", [128, 2, S])
    msk_d = dr("msk", [128, 8, 512])
    cst_d = dr("cst", [128, 3, 128])
    id4_d = dr("id4", [128, 512])
    nmo, ndi = (n_layers + 1) // 2, max(1, n_layers // 2)
    ada_w_d = dr("ada_w", [n_layers, D, 6 * D])
    mwin_d = dr("moba_w_in", [nmo, D, 3 * D])
    mwout_d = dr("moba_w_out", [nmo, D, D])
    dwin_d = dr("diff_w_in", [ndi, D, 3 * D])
    dwout_d = dr("diff_w_out", [ndi, D, D])
    wup_d = dr("mlp_w_up", [n_layers, D, DFF])
    wdn_d = dr("mlp_w_down", [n_layers, DFF, D])
    out_d = dr("outT", [128, 8, S], kind="ExternalOutput")

    st = contextlib.ExitStack()
    with st:
        def sb(name, shape, dt):
            return st.enter_context(nc.sbuf_tensor("sb_" + name, shape, dt))

        X = sb("X", [128, 8, S], F32)
        hT = sb("hT", [128, 8, S], BF16)
        CS = sb("CS", [128, 2, S], F32)
        MSK = sb("MSK", [128, 8, 512], BF16)
        CST = sb("CST", [128, 3, 128], BF16)
        vec = sb("vec", [128, NV], F32)
        mod = sb("mod", [128, DEPTH * 48], F32)
        dv = sb("dv", [128, 2 * DEPTH, 6, 8], F32)
        scb = sb("scb", [128, 8], BF16)
        lamv = sb("lamv", [128, 16], F32)
        lamt = sb("lamt", [128, 2, 64], F32)
        t1 = [sb("t1_0", [128, 512], F32)] * 2
        t2 = [sb("t2_0", [128, 512], F32)] * 2
        fin = [sb("fin_%d" % i, [128, 512], F32) for i in range(2)]
        lnx = [sb("lnx_%d" % i, [128, 512], BF16) for i in range(2)]
        lnq = [sb("lnq_%d" % i, [128, 512], BF16) for i in range(2)]
        gsm = sb("gsm", [128, 96], F32)
        gmA = sb("gmA", [128, 16, 8], F32)
        topA = sb("topA", [128, 16, 8], F32)
        notA = sb("notA", [128, 16, 8], F32)
        kmb = sb("kmb", [128, 128], BF16)
        ID4 = sb("ID4", [128, 512], BF16)
        bq3 = sb("bq3", [128, 8, 16], F32)
        Wada = sb("Wada", [128, 8, 256], BF16)
        ARENA = 29760
        arena = sb("arena", [128, ARENA], BF16)
        ps = [st.enter_context(nc.psum_tensor("ps%d" % i, [128, 512], F32)) for i in range(8)]

        off = [0]

        def carve(n):
            a = off[0]
            off[0] += n
            return a

        def view2(a, n):
            return arena[:, a:a + n]

        def view3(a, n0, n1):
            return arena[:, a:a + n0 * n1].rearrange("p (a b) -> p a b", a=n0)

        Wp = [view3(carve(3072), 8, 384) for _ in range(2)]
        Wout = [view2(carve(1024), 1024) for _ in range(2)]
        qT = [view2(carve(2048), 2048) for _ in range(2)]
        kT = [view2(carve(2048), 2048) for _ in range(2)]
        Vp = [view3(carve(2048), 16, 128) for _ in range(2)]
        oT = [view2(carve(2048), 2048) for _ in range(2)]
        qsb = [view2(carve(512), 512) for _ in range(2)]
        PT = [view2(carve(512), 512) for _ in range(4)]
        Dt = [view2(carve(512), 512) for _ in range(4)]
        assert off[0] <= ARENA
        off[0] = 0
        Wup = [view3(carve(4096), 8, 512) for _ in range(2)]
        Wdn = [view3(carve(4096), 4, 1024) for _ in range(2)]
        uT = [view3(carve(2048), 4, 512) for _ in range(2)]

        ident = CST[:, 0, :]
        ones = CST[:, 1, :]
        Pm = CST[:, 2, :]
        Ctab = CS[:, 0, :]
        Stab = CS[:, 1, :]

        P = Prog(nc)
        gb = {"i": 0, "n": 2}

        def gbank():
            b = gb["i"] % gb["n"]
            gb["i"] += 1
            return b

        def trs(tr):
            return slice(tr * 512, (tr + 1) * 512)

        def pk(bank):
            return [("ps", bank, 0), ("ps", bank, 1)]

        def barrier():
            P.dve(lambda e: e.memset(gsm[:, 90:91], 0.0), w=["arena"])

        for c in range(8):
            P.dma("sp", lambda e, c=c: e.dma_start(out=X[:, c, :], in_=xT_d[:, c, :]),
                  w=[("X", c, tr) for tr in range(4)])
        P.dma("sp", lambda e: e.dma_start(out=vec[:], in_=vec_d), w=["vec"])
        P.dma("sp", lambda e: e.dma_start(out=CS[:], in_=cs_d), w=["CS"])
        P.dma("pool", lambda e: e.dma_start(out=MSK[:], in_=msk_d), w=["MSK"])
        P.dma("pool", lambda e: e.dma_start(out=CST[:], in_=cst_d), w=["CST"])
        P.dma("pool", lambda e: e.dma_start(out=ID4[:], in_=id4_d), w=["ID4"])
        P.dve(lambda e: e.memset(kmb[:], 0.0), w=["kmb"])
        P.act(lambda e: e.activation(out=scb[:], in_=vec[:, V_C:V_C + 8], func=AF.Silu), r=["vec"], w=["scb"])

        def adaln_chunk(l, n):
            bank = 7
            awv = ada_w_d[l].rearrange("(kc p) n -> p kc n", p=128)
            P.dma("pool", lambda e: e.dma_start(out=Wada[:], in_=awv[:, :, n * 256:(n + 1) * 256]), w=["wada"])
            for j in range(2):
                col = n * 2 + j
                for kc in range(8):
                    last = (j == 1 and kc == 7)
                    P.pe(lambda e, j=j, kc=kc, col=col: e.matmul(
                        ps[bank][:, col:col + 1], lhsT=Wada[:, kc, j * 128:(j + 1) * 128],
                        rhs=scb[:, kc:kc + 1], start=(kc == 0), stop=(kc == 7)),
                        r=(["wada", "scb"] if last else ["scb"]), r2=([] if last else ["wada"]),
                        w=[("adaps", col)] + (pk(bank) if n == 0 and j == 0 and kc == 0 else []))
            if n == 23:
                P.dve(lambda e: e.tensor_tensor(out=mod[:, l * 48:(l + 1) * 48], in0=ps[bank][:, 0:48],
                                                in1=vec[:, V_ADAB + l * 48:V_ADAB + (l + 1) * 48], op=ALU.add),
                      r=pk(bank) + [("adaps", c2) for c2 in range(48)] + ["vec"], w=[("mod", l)] + pk(bank))

        def adaln(l):
            for n in range(24):
                adaln_chunk(l, n)

        def derive(l, s):
            k = l * 2 + s
            base = l * 48 + s * 24
            shift = mod[:, base:base + 8]
            scale = mod[:, base + 8:base + 16]
            gate = mod[:, base + 16:base + 24]
            if k == 0:
                gp = vec[:, V_ONE:V_ONE + 8]
                bp = vec[:, V_ZERO:V_ZERO + 8]
            else:
                gp = vec[:, V_LNG + (k - 1) * 8:V_LNG + k * 8]
                bp = vec[:, V_LNB + (k - 1) * 8:V_LNB + k * 8]
            key = ("dv", k)
            rr = [("mod", l), "vec"]
            P.dve(lambda e: e.tensor_single_scalar(out=dv[:, k, 0, :], in_=scale, scalar=1.0, op=ALU.add), r=rr, w=[(key, 0)])
            P.dve(lambda e: e.tensor_tensor(out=dv[:, k, 1, :], in0=gp, in1=dv[:, k, 0, :], op=ALU.mult), r=rr + [(key, 0)], w=[(key, 1)])
            P.dve(lambda e: e.tensor_tensor(out=dv[:, k, 2, :], in0=bp, in1=dv[:, k, 0, :], op=ALU.mult), r=rr + [(key, 0)], w=[(key, 2)])
            P.dve(lambda e: e.tensor_tensor(out=dv[:, k, 2, :], in0=dv[:, k, 2, :], in1=shift, op=ALU.add), r=rr + [(key, 2)], w=[(key, 2)])
            P.dve(lambda e: e.tensor_single_scalar(out=dv[:, k, 3, :], in_=gp, scalar=ALPHA, op=ALU.mult), r=rr, w=[(key, 3)])
            P.dve(lambda e: e.tensor_single_scalar(out=dv[:, k, 4, :], in_=bp, scalar=ALPHA, op=ALU.mult), r=rr, w=[(key, 4)])
            P.dve(lambda e: e.tensor_single_scalar(out=dv[:, k, 5, :], in_=gate, scalar=1.0, op=ALU.add), r=rr, w=[(key, 5)])

        def make_h(k):
            key = ("dv", k)
            for c in range(8):
                for tr in range(4):
                    P.act(lambda e, c=c, tr=tr: e.activation(
                        out=hT[:, c, trs(tr)], in_=X[:, c, trs(tr)], func=AF.Identity,
                        bias=dv[:, k, 2, c:c + 1], scale=dv[:, k, 1, c:c + 1]),
                        r=[("X", c, tr), (key, 1), (key, 2)], w=[("h", c, tr)])
                    P.dve(lambda e, c=c, tr=tr: e.tensor_scalar(
                        out=X[:, c, trs(tr)], in0=X[:, c, trs(tr)], scalar1=dv[:, k, 3, c:c + 1],
                        scalar2=dv[:, k, 4, c:c + 1], op0=ALU.mult, op1=ALU.add),
                        r=[("X", c, tr), (key, 3), (key, 4)], w=[("X", c, tr)])

        def layer_norm():
            gb["n"] = 2
            for tr in range(4):
                bm, bq = 2 + (tr % 2) * 2, 3 + (tr % 2) * 2
                for c in range(8):
                    i = c % 2
                    P.act(lambda e, c=c, tr=tr, i=i: e.copy(out=lnx[i][:], in_=X[:, c, trs(tr)]),
                          r=[("X", c, tr)], w=[("lnx", i)])
                    P.dve(lambda e, c=c, tr=tr, i=i: e.tensor_tensor(out=lnq[i][:], in0=X[:, c, trs(tr)],
                                                                    in1=X[:, c, trs(tr)], op=ALU.mult),
                          r=[("X", c, tr)], w=[("lnq", i)])
                    P.pe(lambda e, c=c, i=i, bm=bm: e.matmul(ps[bm][:], lhsT=ones, rhs=lnx[i][:], start=(c == 0), stop=(c == 7)),
                         r=[("lnx", i), "CST"], w=pk(bm))
                    P.pe(lambda e, c=c, i=i, bq=bq: e.matmul(ps[bq][:], lhsT=ones, rhs=lnq[i][:], start=(c == 0), stop=(c == 7)),
                         r=[("lnq", i), "CST"], w=pk(bq))
                mu, rs = fin[0], fin[1]
                P.dve(lambda e, bm=bm: e.tensor_single_scalar(out=mu[:], in_=ps[bm][:], scalar=1.0 / D, op=ALU.mult),
                      r=pk(bm), w=[("fin", 0)])
                P.dve(lambda e: e.tensor_tensor(out=rs[:], in0=mu[:], in1=mu[:], op=ALU.mult), r=[("fin", 0)], w=[("fin", 1)])
                P.dve(lambda e, bq=bq: e.scalar_tensor_tensor(out=rs[:], in0=ps[bq][:], scalar=1.0 / D, in1=rs[:],
                                                             op0=ALU.mult, op1=ALU.subtract),
                      r=pk(bq) + [("fin", 1)], w=[("fin", 1)])
                P.act(lambda e: e.activation(out=rs[:], in_=rs[:], func=AF.Ln, bias=vec[:, V_EPS:V_EPS + 1], scale=1.0),
                      r=[("fin", 1), "vec"], w=[("fin", 1)])
                P.act(lambda e: e.activation(out=rs[:], in_=rs[:], func=AF.Exp, scale=-0.5), r=[("fin", 1)], w=[("fin", 1)])
                for c in range(8):
                    P.dve(lambda e, c=c, tr=tr: e.tensor_tensor(out=X[:, c, trs(tr)], in0=X[:, c, trs(tr)], in1=mu[:],
                                                               op=ALU.subtract),
                          r=[("X", c, tr), ("fin", 0)], w=[("X", c, tr)])
                    P.dve(lambda e, c=c, tr=tr: e.tensor_tensor(out=X[:, c, trs(tr)], in0=X[:, c, trs(tr)], in1=rs[:],
                                                               op=ALU.mult),
                          r=[("X", c, tr), ("fin", 1)], w=[("X", c, tr)])

        def attention(l, kind):
            k = l * 2
            j = l // 2
            gb["n"] = 2
            win_d = (mwin_d if kind == "moba" else dwin_d)[j].rearrange("(kc p) n -> p kc n", p=128)
            wout_d = (mwout_d if kind == "moba" else dwout_d)[j]
            GT = dv[:, k, 5, :]
            if kind == "diff":
                lam_init = 0.8 - 0.6 * math.exp(-0.3 * l)
                lb = V_LAM + j * 256
                for i2 in range(2):
                    P.dve(lambda e, i2=i2: e.tensor_tensor(out=lamt[:, i2, :], in0=vec[:, lb + i2 * 128:lb + i2 * 128 + 64],
                                                          in1=vec[:, lb + i2 * 128 + 64:lb + i2 * 128 + 128], op=ALU.mult),
                          r=["vec"], w=[("lamt", i2)])
                P.dve(lambda e: e.tensor_reduce(out=lamv[:, 2:4], in_=lamt[:], axis=AX.X, op=ALU.add),
                      r=[("lamt", 0), ("lamt", 1)], w=["lamv23"])
                P.act(lambda e: e.activation(out=lamv[:, 4:6], in_=lamv[:, 2:4], func=AF.Exp), r=["lamv23"], w=["lamv45"])
                P.dve(lambda e: e.scalar_tensor_tensor(out=lamv[:, 0:1], in0=lamv[:, 5:6], scalar=-lam_init, in1=lamv[:, 4:5],
                                                       op0=ALU.add, op1=ALU.subtract),
                      r=["lamv45"], w=["neglam"])
                P.dve(lambda e: e.tensor_single_scalar(out=lamv[:, 1:2], in_=vec[:, V_SUBG + j:V_SUBG + j + 1],
                                                       scalar=(1.0 - lam_init), op=ALU.mult),
                      r=["vec"], w=["gs"])

            pending = []

            def flush(force=False):
                keep = []
                for item in pending:
                    item[0] -= 1
                    if force or item[0] <= 0:
                        item[1]()
                    else:
                        keep.append(item)
                pending[:] = keep

            def do_pair(p):
                b = p % 2
                for which in range(3):
                    P.dma("pool", lambda e, b=b, which=which, p=p: e.dma_start(
                        out=Wp[b][:, :, which * 128:(which + 1) * 128],
                        in_=win_d[:, :, which * 1024 + p * 128:which * 1024 + (p + 1) * 128]),
                        r=["arena"], w=[("wp", b, which)])
                P.dma("pool", lambda e, b=b, p=p: e.dma_start(out=Wout[b], in_=wout_d[p * 128:(p + 1) * 128, :]),
                      r=["arena"], w=[("wout", b)])
                if SUB < 1:
                    return
                for which, dst, dkey in ((0, qT[b], "qT"), (1, kT[b], "kT")):
                    for tr in range(4):
                        bank = gbank()
                        for kc in range(8):
                            P.pe(lambda e, b=b, which=which, tr=tr, kc=kc, bank=bank: e.matmul(
                                ps[bank][:], lhsT=Wp[b][:, kc, which * 128:(which + 1) * 128], rhs=hT[:, kc, trs(tr)],
                                start=(kc == 0), stop=(kc == 7)),
                                r=[("wp", b, which), ("h", kc, tr), "arena"], w=pk(bank))
                        i = tr % 2
                        P.act(lambda e, i=i, bank=bank: e.copy(out=qsb[i], in_=ps[bank][:]),
                              r=pk(bank) + ["arena"], w=[("qsb", i)])
                        if not ROPE:
                            P.act(lambda e, bank=bank, dst=dst, tr=tr: e.copy(out=dst[:, trs(tr)], in_=ps[bank][:]),
                                  r=pk(bank) + ["arena"], w=[(dkey, b, tr)])
                            continue
                        bank2 = gbank()
                        P.pe(lambda e, i=i, bank2=bank2: e.matmul(ps[bank2][:], lhsT=Pm, rhs=qsb[i], start=True, stop=True),
                             r=[("qsb", i), "CST"], w=pk(bank2))
                        P.dve(lambda e, i=i, bank=bank, tr=tr: e.tensor_tensor(out=t1[i][:], in0=ps[bank][:], in1=Ctab[:, trs(tr)],
                                                                              op=ALU.mult),
                              r=pk(bank) + ["CS", ("qsb", i)], w=[("t1", 0)])
                        P.dve(lambda e, i=i, bank2=bank2, tr=tr: e.tensor_tensor(out=t2[i][:], in0=ps[bank2][:],
                                                                                in1=Stab[:, trs(tr)], op=ALU.mult),
                              r=pk(bank2) + ["CS"], w=[("t2", 0)])
                        P.dve(lambda e, i=i, dst=dst, tr=tr: e.tensor_tensor(out=dst[:, trs(tr)], in0=t1[i][:], in1=t2[i][:],
                                                                            op=ALU.add),
                              r=[("t1", 0), ("t2", 0), "arena"], w=[(dkey, b, tr)])
                if SUB < 2:
                    return
                for g4 in range(4):
                    bank = gbank()
                    for tt4 in range(4):
                        tt = g4 * 4 + tt4
                        for kc in range(8):
                            P.pe(lambda e, b=b, tt=tt, tt4=tt4, kc=kc, bank=bank: e.matmul(
                                ps[bank][:, tt4 * 128:(tt4 + 1) * 128], lhsT=hT[:, kc, tt * 128:(tt + 1) * 128],
                                rhs=Wp[b][:, kc, 256:384], start=(kc == 0), stop=(kc == 7)),
                                r=[("wp", b, 2), ("h", kc, g4), "arena"], w=pk(bank))
                    for tt4 in range(4):
                        P.act(lambda e, b=b, g4=g4, tt4=tt4, bank=bank: e.copy(
                            out=Vp[b][:, g4 * 4 + tt4, :], in_=ps[bank][:, tt4 * 128:(tt4 + 1) * 128]),
                            r=pk(bank) + ["arena"], w=[("V", b, g4, tt4)])
                flush(force=True)
                if SUB < 3:
                    return
                if kind == "moba" and not NODYN:
                    P.dve(lambda e, b=b: e.tensor_reduce(out=gsm[:, 48:56], in_=kT[b].rearrange("p (a b) -> p a b", a=8),
                                                         axis=AX.X, op=ALU.add),
                          r=[("kT", b, tr) for tr in range(4)], w=["kms"])
                    P.dve(lambda e: e.tensor_single_scalar(out=kmb[0:64, 0:8], in_=gsm[0:64, 48:56], scalar=1.0 / 256, op=ALU.mult),
                          r=["kms"], w=["kmb"])
                    P.dve(lambda e: e.tensor_single_scalar(out=kmb[64:128, 8:16], in_=gsm[64:128, 48:56], scalar=1.0 / 256,
                                                           op=ALU.mult),
                          r=["kms", "kmb"], w=["kmb"])
                    for qt in range(8):
                        gbk = 4 + qt // 4
                        P.pe(lambda e, b=b, qt=qt, gbk=gbk: e.matmul(
                            ps[gbk][:, (qt % 4) * 128:(qt % 4 + 1) * 128],
                            lhsT=qT[b][:, (8 + qt) * 128:(9 + qt) * 128], rhs=kmb[:], start=True, stop=True),
                            r=[("qT", b, (8 + qt) // 4), "kmb", "arena"], w=pk(gbk))
                    for g in range(16):
                        gbk = 4 + (g // 2) // 4
                        P.dve(lambda e, g=g, gbk=gbk: e.tensor_tensor(
                            out=gmA[:, g, :], in0=ps[gbk][:, ((g // 2) % 4) * 128 + (g % 2) * 8:((g // 2) % 4) * 128 + (g % 2) * 8 + 8],
                            in1=vec[:, V_PASTNEG + g * 8:V_PASTNEG + g * 8 + 8], op=ALU.add),
                            r=pk(gbk) + ["vec"], w=[("gm", g)])
                        P.dve(lambda e, g=g: e.max(out=topA[:, g, :], in_=gmA[:, g, :]), r=[("gm", g)], w=[("top8", g)])
                        P.dve(lambda e, g=g: e.tensor_scalar(out=notA[:, g, :], in0=gmA[:, g, :], scalar1=topA[:, g, 2:3],
                                                             scalar2=None, op0=ALU.is_lt),
                              r=[("gm", g), ("top8", g)], w=[("nots", g)])
                    P.dve(lambda e: e.tensor_tensor(out=bq3[:].rearrange("p a b -> p (a b)"),
                                                    in0=notA[:].rearrange("p a b -> p (a b)"),
                                                    in1=vec[:, V_NEGPAST:V_NEGPAST + 128], op=ALU.mult),
                          r=[("nots", g) for g in range(16)] + ["vec"], w=[("bq3", qt) for qt in range(8)])
                if SUB == 3.5:
                    bankd = 6
                    P.pe(lambda e, bankd=bankd: e.matmul(ps[bankd][:, 0:128], lhsT=ident, rhs=ones, start=True, stop=True),
                         r=["CST"] + ([("bq3", 0)] if DBG_DEST == 0 else []), w=pk(bankd))
                    P.dve(lambda e: e.memset(gsm[:, 91:92], 0.0), r=pk(bankd), w=[("X", c, tr) for c in range(8) for tr in range(4)])
                if SUB < 4:
                    return

                def emit_pv(qr, kt, hh, pti, first, last):
                    if kind == "moba":
                        ob, sbk = (4, 5) if qr % 2 == 0 else (6, 7)
                        rows = slice(hh * 64, (hh + 1) * 64)
                        P.pe(lambda e: e.matmul(ps[ob][rows, :], lhsT=Vp[b][:, kt, hh * 64:(hh + 1) * 64], rhs=PT[pti],
                                                start=first, stop=last),
                             r=[("V", b, kt // 4, kt % 4), ("PT", pti), "arena"], w=[("ps", ob, hh)])
                        P.pe(lambda e: e.matmul(ps[sbk][rows, :], lhsT=ones[:, 0:64], rhs=PT[pti], start=first, stop=last),
                             r=[("PT", pti), "CST"], w=[("ps", sbk, hh)])
                    else:
                        ob, sbk = 4 + 2 * hh, 5 + 2 * hh
                        P.pe(lambda e: e.matmul(ps[ob][:], lhsT=Vp[b][:, kt, :], rhs=PT[pti], start=first, stop=last),
                             r=[("V", b, kt // 4, kt % 4), ("PT", pti), "arena"], w=pk(ob))
                        P.pe(lambda e: e.matmul(ps[sbk][:], lhsT=ones, rhs=PT[pti], start=first, stop=last),
                             r=[("PT", pti), "CST"], w=pk(sbk))

                def finalize(qr):
                    tr = qr
                    if kind == "moba":
                        ob, sbk = (4, 5) if qr % 2 == 0 else (6, 7)
                        f = fin[qr % 2]
                        P.dve(lambda e: e.reciprocal(out=f[:], in_=ps[sbk][:]),
                              r=pk(sbk), w=[("fin", qr % 2)])
                        P.dve(lambda e: e.tensor_tensor(out=oT[b][:, trs(tr)], in0=ps[ob][:], in1=f[:], op=ALU.mult),
                              r=pk(ob) + [("fin", qr % 2), "arena"], w=[("oT", b, tr)])
                    else:
                        f0, f1 = fin[0], fin[1]
                        P.dve(lambda e: e.reciprocal(out=f0[:], in_=ps[5][:]), r=pk(5), w=[("fin", 0)])
                        P.dve(lambda e: e.tensor_tensor(out=f0[:], in0=ps[4][:], in1=f0[:], op=ALU.mult),
                              r=pk(4) + [("fin", 0)], w=[("fin", 0)])
                        P.dve(lambda e: e.reciprocal(out=f1[:], in_=ps[7][:]), r=pk(7), w=[("fin", 1)])
                        P.dve(lambda e: e.tensor_tensor(out=f1[:], in0=ps[6][:], in1=f1[:], op=ALU.mult),
                              r=pk(6) + [("fin", 1)], w=[("fin", 1)])
                        P.dve(lambda e: e.scalar_tensor_tensor(out=f0[:], in0=f1[:], scalar=lamv[:, 0:1], in1=f0[:],
                                                               op0=ALU.mult, op1=ALU.add),
                              r=[("fin", 0), ("fin", 1), "neglam"], w=[("fin", 0)])
                        P.dve(lambda e: e.tensor_tensor(out=lnq[0][:], in0=f0[:], in1=f0[:], op=ALU.mult),
                              r=[("fin", 0)], w=[("lnq", 0)])
                        bank = gbank()
                        P.pe(lambda e: e.matmul(ps[bank][:], lhsT=ones, rhs=lnq[0][:], start=True, stop=True),
                             r=[("lnq", 0), "CST"], w=pk(bank))
                        P.act(lambda e: e.activation(out=f1[:], in_=ps[bank][:], func=AF.Ln, bias=vec[:, V_EPS:V_EPS + 1],
                                                     scale=1.0 / 128),
                              r=pk(bank) + ["vec", ("fin", 1)], w=[("fin", 1)])
                        P.act(lambda e: e.activation(out=f1[:], in_=f1[:], func=AF.Exp, scale=-0.5),
                              r=[("fin", 1)], w=[("fin", 1)])
                        P.dve(lambda e: e.scalar_tensor_tensor(out=oT[b][:, trs(tr)], in0=f0[:], scalar=lamv[:, 1:2], in1=f1[:],
                                                               op0=ALU.mult, op1=ALU.mult),
                              r=[("fin", 0), ("fin", 1), "gs", "arena"], w=[("oT", b, tr)])

                def outproj(qr):
                    tr = qr
                    if SUB < 5:
                        return
                    for oc in range(8):
                        bank = gbank()
                        P.pe(lambda e, oc=oc, bank=bank: e.matmul(ps[bank][:], lhsT=Wout[b][:, oc * 128:(oc + 1) * 128],
                                                                  rhs=oT[b][:, trs(tr)], start=True, stop=True),
                             r=[("wout", b), ("oT", b, tr), "arena"], w=pk(bank))
                        P.dve(lambda e, oc=oc, bank=bank: e.scalar_tensor_tensor(
                            out=X[:, oc, trs(tr)], in0=ps[bank][:], scalar=GT[:, oc:oc + 1], in1=X[:, oc, trs(tr)],
                            op0=ALU.mult, op1=ALU.add),
                            r=pk(bank) + [("X", oc, tr), (("dv", k), 5)], w=[("X", oc, tr)])

                prev = []
                pti_ctr = [0]
                for qr in range(4):
                    nkt = 4 * qr + 4
                    for kt in range(nkt):
                        cur = []
                        for hh in range(2):
                            sbank = 2 + hh
                            pti = pti_ctr[0] % 4
                            pti_ctr[0] += 1
                            jd = kt - 4 * qr
                            extra = []
                            if jd >= 0:
                                extra.append(("static", (0 if kind == "moba" else 4) + jd))
                            if kind == "moba" and not NODYN and qr >= 2 and (kt // 2) < 2 * qr + 1:
                                nblk = kt // 2
                                slot = hh * 2 + (nblk % 2)
                                if kt % 2 == 0:
                                    col = hh * 8 + nblk
                                    for q4 in range(4):
                                        P.dve(lambda e, slot=slot, col=col, qr=qr, q4=q4: e.tensor_scalar(
                                            out=Dt[slot][:, q4 * 128:(q4 + 1) * 128], in0=ident,
                                            scalar1=bq3[:, (qr - 2) * 4 + q4, col:col + 1], scalar2=None, op0=ALU.mult),
                                            r=[("bq3", (qr - 2) * 4 + q4), "CST", "arena"], w=[("D", slot, q4)])
                                extra.append(("dyn", slot))
                            nmm = 1 + len(extra)
                            P.pe(lambda e, hh=hh, kt=kt, qr=qr, sbank=sbank, nmm=nmm: e.matmul(
                                ps[sbank][:], lhsT=kT[b][hh * 64:(hh + 1) * 64, kt * 128:(kt + 1) * 128],
                                rhs=qT[b][hh * 64:(hh + 1) * 64, trs(qr)], start=True, stop=(nmm == 1)),
                                r=[("kT", b, kt // 4), ("qT", b, qr), "arena"], w=pk(sbank))
                            for xi, (xk, xv) in enumerate(extra):
                                lastx = (xi == len(extra) - 1)
                                if xk == "static":
                                    P.pe(lambda e, xv=xv, sbank=sbank, lastx=lastx: e.matmul(
                                        ps[sbank][:], lhsT=ident, rhs=MSK[:, xv, :], start=False, stop=lastx),
                                        r=["MSK", "CST"], w=pk(sbank))
                                else:
                                    P.pe(lambda e, xv=xv, sbank=sbank, lastx=lastx: e.matmul(
                                        ps[sbank][:], lhsT=ones, rhs=Dt[xv], start=False, stop=lastx),
                                        r=[("D", xv, q4) for q4 in range(4)] + ["CST", "arena"], w=pk(sbank))
                            P.act(lambda e, sbank=sbank, pti=pti: e.activation(out=PT[pti], in_=ps[sbank][:], func=AF.Exp,
                                                                                scale=0.125),
                                  r=pk(sbank) + ["arena"], w=[("PT", pti)])
                            cur.append((qr, kt, hh, pti, kt == 0, kt == nkt - 1))
                        for a in prev:
                            emit_pv(*a)
                        if prev and prev[0][5]:
                            pqr = prev[0][0]
                            finalize(pqr)
                            pending.append([2, (lambda pqr=pqr: outproj(pqr))])
                        prev = cur
                        flush()
                for a in prev:
                    emit_pv(*a)
                finalize(3)
                pending.append([1, (lambda: outproj(3))])

            for p in range(NPAIRS):
                do_pair(p)
            flush(force=True)

        def mlp(l, nxt_ada=None):
            k = l * 2 + 1
            GT = dv[:, k, 5, :]
            gb["n"] = 2
            upv = wup_d[l].rearrange("(kc p) n -> p kc n", p=128)
            dnv = wdn_d[l].rearrange("(fc p) n -> p fc n", p=128)
            ubank = [0]
            ybank = [0]

            def up(g, tr, ui):
                b = g % 2
                for fc in range(4):
                    bank = ubank[0] % 4
                    ubank[0] += 1
                    for kc in range(8):
                        P.pe(lambda e, fc=fc, kc=kc, bank=bank: e.matmul(
                            ps[bank][:], lhsT=Wup[b][:, kc, fc * 128:(fc + 1) * 128], rhs=hT[:, kc, trs(tr)],
                            start=(kc == 0), stop=(kc == 7)),
                            r=[("wup", b), ("h", kc, tr), "arena"], w=pk(bank))
                    ti = fc % 2
                    P.act(lambda e, bank=bank, ti=ti: e.activation(out=t1[ti][:], in_=ps[bank][:], func=AF.Relu),
                          r=pk(bank), w=[("t1", 0)])
                    P.dve(lambda e, fc=fc, ti=ti: e.tensor_tensor(out=uT[ui][:, fc, :], in0=t1[ti][:], in1=t1[ti][:], op=ALU.mult),
                          r=[("t1", 0), "arena"], w=[("uT", ui, fc)])

            def down(g, tr, ui):
                b = g % 2
                for oc in range(8):
                    bank = 4 + ybank[0] % 3
                    ybank[0] += 1
                    for fc in range(4):
                        P.pe(lambda e, fc=fc, oc=oc, bank=bank: e.matmul(
                            ps[bank][:], lhsT=Wdn[b][:, fc, oc * 128:(oc + 1) * 128], rhs=uT[ui][:, fc, :],
                            start=(fc == 0), stop=(fc == 3)),
                            r=[("wdn", b), ("uT", ui, fc), "arena"], w=pk(bank))
                    P.dve(lambda e, oc=oc, bank=bank: e.scalar_tensor_tensor(
                        out=X[:, oc, trs(tr)], in0=ps[bank][:], scalar=GT[:, oc:oc + 1], in1=X[:, oc, trs(tr)],
                        op0=ALU.mult, op1=ALU.add),
                        r=pk(bank) + [("X", oc, tr), (("dv", k), 5)], w=[("X", oc, tr)])

            prev = None
            step = 0
            for g in range(8):
                b = g % 2
                P.dma("pool", lambda e, b=b, g=g: e.dma_start(out=Wup[b], in_=upv[:, :, g * 512:(g + 1) * 512]),
                      r=["arena"], w=[("wup", b)])
                P.dma("pool", lambda e, b=b, g=g: e.dma_start(out=Wdn[b], in_=dnv[:, g * 4:(g + 1) * 4, :]),
                      r=["arena"], w=[("wdn", b)])
                for tr in range(4):
                    ui = step % 2
                    step += 1
                    up(g, tr, ui)
                    if prev is not None:
                        down(*prev)
                    prev = (g, tr, ui)
                    if nxt_ada is not None and step <= 24:
                        adaln_chunk(nxt_ada, step - 1)
            down(*prev)

        adaln(0)
        for l in range(n_layers):
            kind = "moba" if l % 2 == 0 else "diff"
            derive(l, 0)
            derive(l, 1)
            if STAGE < 2:
                break
            barrier()
            make_h(2 * l)
            if STAGE < 3:
                break
            attention(l, kind)
            if STAGE < 4:
                break
            layer_norm()
            if STAGE < 5:
                break
            barrier()
            make_h(2 * l + 1)
            if STAGE < 6:
                break
            mlp(l, (l + 1) if l + 1 < n_layers else None)
            if STAGE < 7:
                break
            layer_norm()
        kl = 2 * n_layers - 1
        outs = []
        for c in range(8):
            for tr in range(4):
                P.dve(lambda e, c=c, tr=tr: e.tensor_scalar(
                    out=X[:, c, trs(tr)], in0=X[:, c, trs(tr)], scalar1=vec[:, V_LNG + kl * 8 + c:V_LNG + kl * 8 + c + 1],
                    scalar2=vec[:, V_LNB + kl * 8 + c:V_LNB + kl * 8 + c + 1], op0=ALU.mult, op1=ALU.add),
                    r=[("X", c, tr), "vec"], w=[("X", c, tr)])
            outs.append(P.dma("sp", lambda e, c=c: e.dma_start(out=out_d[:, c, :], in_=X[:, c, :]),
                              r=[("X", c, tr) for tr in range(4)]))
        P.emit(final_wait_ops=outs)
    return nc


def _consts():
    pos = np.arange(S, dtype=np.float32)
    inv = (np.float32(ROPE_THETA) ** (-np.arange(0, 16, 2, dtype=np.float32) / np.float32(16))).astype(np.float32)
    ang = (pos[:, None] * inv[None, :]).astype(np.float32)
    cos = np.cos(ang).astype(np.float32)
    sin = np.sin(ang).astype(np.float32)
    cs = np.zeros((128, 2, S), np.float32)
    cs[:, 0, :] = 1.0
    for hh in range(2):
        for d in range(8):
            cs[hh * 64 + d, 0, :] = cos[:, d]
            cs[hh * 64 + 8 + d, 0, :] = cos[:, d]
            cs[hh * 64 + d, 1, :] = -sin[:, d]
            cs[hh * 64 + 8 + d, 1, :] = sin[:, d]
    msk = np.zeros((128, 8, 512), np.float32)
    kk = np.arange(128)[:, None]
    qq = np.arange(512)[None, :]
    for j in range(4):
        kp = j * 128 + kk
        same0 = (kp < 256) & (qq < 256) & (kp <= qq)
        past = (kp < 256) & (qq >= 256)
        same1 = (kp >= 256) & (qq >= 256) & (kp <= qq)
        msk[:, j, :] = np.where(same0 | past | same1, 0.0, NEG)
        msk[:, 4 + j, :] = np.where(kp <= qq, 0.0, NEG)
    cst = np.zeros((128, 3, 128), np.float32)
    cst[:, 0, :] = np.eye(128, dtype=np.float32)
    cst[:, 1, :] = 1.0
    for hh in range(2):
        for d in range(8):
            cst[hh * 64 + d + 8, 2, hh * 64 + d] = 1.0
            cst[hh * 64 + d, 2, hh * 64 + d + 8] = 1.0
    return cs, msk, cst


def _vec(c_b, ada_b, ln_g, ln_b, subln_g, lams):
    v = np.zeros((128, NV), np.float32)
    v[:, V_C:V_C + 8] = c_b.reshape(8, 128).T
    v[:, V_ADAB:V_ADAB + 192] = ada_b.reshape(DEPTH, 48, 128).transpose(2, 0, 1).reshape(128, 192)
    v[:, V_LNG:V_LNG + 64] = ln_g.reshape(DEPTH * 2, 8, 128).transpose(2, 0, 1).reshape(128, 64)
    v[:, V_LNB:V_LNB + 64] = ln_b.reshape(DEPTH * 2, 8, 128).transpose(2, 0, 1).reshape(128, 64)
    v[:, V_SUBG:V_SUBG + 2] = subln_g.T
    for qt in range(8):
        qb = 4 + qt // 2
        for hh in range(2):
            for n in range(8):
                v[:, V_PASTNEG + qt * 16 + hh * 8 + n] = 0.0 if n < qb else -1e30
                v[:, V_NEGPAST + qt * 16 + hh * 8 + n] = NEG if n < qb else 0.0
    v[:, V_ONE:V_ONE + 8] = 1.0
    v[:, V_EPS] = LN_EPS
    for j in range(2):
        for i, a in enumerate(lams):
            v[:, V_LAM + j * 256 + i * 64:V_LAM + j * 256 + (i + 1) * 64] = a[j][None, :]
    return v


_NC_CACHE = {}
STAGE = 99


def _run(inputs, n_layers, cores):
    x = np.asarray(inputs["x"], np.float32)
    c = np.asarray(inputs["c"], np.float32)
    cs, msk, cst = _consts()
    f = lambda k: np.ascontiguousarray(np.asarray(inputs[k], np.float32))
    nmo, ndi = (n_layers + 1) // 2, max(1, n_layers // 2)
    nsl = {"ada_w": n_layers, "mlp_w_up": n_layers, "mlp_w_down": n_layers, "moba_w_in": nmo, "moba_w_out": nmo,
           "diff_w_in": ndi, "diff_w_out": ndi}
    shared = {k: np.ascontiguousarray(np.asarray(inputs[k], np.float32)[:n]) for k, n in nsl.items()}
    lams = [f("diff_lam_q1"), f("diff_lam_k1"), f("diff_lam_q2"), f("diff_lam_k2")]
    in_maps = []
    for b in cores:
        xT = np.ascontiguousarray(x[b].T.reshape(8, 128, S).transpose(1, 0, 2))
        m = {"xT": xT, "vec": _vec(c[b], f("ada_b"), f("ln_g"), f("ln_b"), f("diff_subln_g"), lams),
             "cs": cs, "msk": msk, "cst": cst, "id4": np.tile(np.eye(128, dtype=np.float32), (1, 4))}
        m.update(shared)
        in_maps.append(m)
    if n_layers not in _NC_CACHE:
        _NC_CACHE[n_layers] = build(n_layers)
    nc = _NC_CACHE[n_layers]
    res = run_bass_kernel_spmd(nc, in_maps, core_ids=list(range(len(cores))))
    outs = []
    for r in res.results:
        oT = r["outT"]
        outs.append(oT.transpose(2, 1, 0).reshape(S, D))
    return np.stack(outs, 0).astype(np.float32)


def kernel(**inputs):
    return _run(inputs, DEPTH, list(range(8)))
```

```python
import contextlib
import math

import numpy as np
import concourse.bass as bass
import concourse.mybir as mybir
from concourse.bass_utils import run_bass_kernel_spmd

F32 = mybir.dt.float32
BF16 = mybir.dt.bfloat16
ALU = mybir.AluOpType
AF = mybir.ActivationFunctionType
AX = mybir.AxisListType

D = 1024
S = 2048
DEPTH = 4
DFF = 4096
ALPHA = (2.0 * DEPTH) ** 0.25
NEG = -30000.0
LN_EPS = 1e-5
ROPE_THETA = 500000.0

ENGS = ("pe", "act", "dve", "pool", "sp")
PSUM_SERIAL = False
DBG_LOG = None
SAME_ENG_DIST = 10 ** 9
N_DMA_SEMS = 24


class Op:
    __slots__ = ("eng", "fn", "deps", "dma", "sig", "cnt", "dsem", "dval", "dprev")

    def __init__(self, eng, fn, deps, dma):
        self.eng = eng
        self.fn = fn
        self.deps = deps
        self.dma = dma
        self.sig = False
        self.cnt = 0
        self.dsem = None
        self.dval = 0
        self.dprev = 0


class Prog:
    def __init__(self, nc):
        self.nc = nc
        self.ops = []
        self.last_w = {}
        self.readers = {}

    def add(self, eng, fn, r=(), w=(), dma=False, r2=()):
        idx = len(self.ops)
        if PSUM_SERIAL and eng in ("act", "dve") and any(isinstance(k, tuple) and k[0] == "ps" for k in r):
            w = list(w) + ["PSRD"]
        deps = set()
        for k in list(r) + list(r2):
            lw = self.last_w.get(k)
            if lw is not None:
                deps.add(lw)
        for k in w:
            lw = self.last_w.get(k)
            if lw is not None:
                deps.add(lw)
            deps.update(self.readers.get(k, ()))
        for k in r:
            lst = self.readers.setdefault(k, [])
            if not dma:
                lst[:] = [j for j in lst if self.ops[j].dma or self.ops[j].eng != eng]
            lst.append(idx)
        for k in w:
            self.last_w[k] = idx
            self.readers[k] = []
        deps.discard(idx)
        self.ops.append(Op(eng, fn, deps, dma))
        return idx

    def pe(self, fn, r=(), w=(), r2=()):
        return self.add("pe", fn, r, w, r2=r2)

    def act(self, fn, r=(), w=()):
        return self.add("act", fn, r, w)

    def dve(self, fn, r=(), w=()):
        return self.add("dve", fn, r, w)

    def dma(self, q, fn, r=(), w=()):
        return self.add(q, fn, r, w, dma=True)

    def emit(self, final_wait_ops=()):
        nc = self.nc
        ops = self.ops
        pos = {}
        ctr = {e: 0 for e in ENGS}
        for i, op in enumerate(ops):
            pos[i] = ctr[op.eng]
            ctr[op.eng] += 1
        self.pos = pos

        def same_eng_skip(i, d):
            op, dop = ops[i], ops[d]
            if op.dma or dop.dma or op.eng != dop.eng:
                return False
            if op.eng == "pe":
                return True
            return op.eng in ("act", "dve") and pos[i] - pos[d] >= SAME_ENG_DIST

        self.same_eng_skip = same_eng_skip
        for i, op in enumerate(ops):
            for d in op.deps:
                dop = ops[d]
                if dop.dma:
                    continue
                if same_eng_skip(i, d):
                    continue
                dop.sig = True
        cnt = {e: 0 for e in ENGS}
        dma_uses = [0] * N_DMA_SEMS
        ndma = {"sp": 0, "pool": 0, "act": 0}
        for op in ops:
            if op.dma:
                if op.eng == "pool":
                    s = 8 + ndma["pool"] % (N_DMA_SEMS - 8)
                else:
                    s = ndma[op.eng] % 8
                ndma[op.eng] += 1
                op.dsem = s
                op.dprev = 16 * dma_uses[s]
                dma_uses[s] += 1
                op.dval = 16 * dma_uses[s]
            elif op.sig:
                cnt[op.eng] += 1
                op.cnt = cnt[op.eng]
        with contextlib.ExitStack() as st:
            esem = {e: st.enter_context(nc.semaphore("s_" + e)) for e in ENGS}
            dsem = [st.enter_context(nc.semaphore("d_%d" % i)) for i in range(N_DMA_SEMS)]
            block = st.enter_context(nc.Block())
            per_eng = {e: [] for e in ENGS}
            for i, op in enumerate(ops):
                per_eng[op.eng].append(i)

            def run_engine(ename, eng):
                waited = {}

                def wait(semkey, sem, val):
                    if waited.get(semkey, 0) >= val:
                        return
                    waited[semkey] = val
                    eng.wait_ge(sem, val)
                    if DBG_LOG is not None:
                        DBG_LOG.append((ename, "wait", semkey, val))

                for i in per_eng[ename]:
                    op = ops[i]
                    for d in sorted(op.deps):
                        dop = ops[d]
                        if dop.dma:
                            wait(("d", dop.dsem), dsem[dop.dsem], dop.dval)
                        else:
                            if self.same_eng_skip(i, d):
                                continue
                            wait(("e", dop.eng), esem[dop.eng], dop.cnt)
                    if op.dma:
                        if op.dprev > 0:
                            wait(("d", op.dsem), dsem[op.dsem], op.dprev)
                        ins = op.fn(eng)
                        ins.then_inc(dsem[op.dsem], 16)
                    else:
                        ins = op.fn(eng)
                        if op.sig:
                            ins.then_inc(esem[ename], 1)
                        if DBG_LOG is not None:
                            DBG_LOG.append((ename, "op", i, op.cnt if op.sig else None, str(ins)[:90]))
                if ename == "sp":
                    for i in final_wait_ops:
                        op = ops[i]
                        wait(("d", op.dsem), dsem[op.dsem], op.dval)

            @block.tensor
            def _(e):
                run_engine("pe", e)

            @block.scalar
            def _(e):
                run_engine("act", e)

            @block.vector
            def _(e):
                run_engine("dve", e)

            @block.gpsimd
            def _(e):
                run_engine("pool", e)

            @block.sync
            def _(e):
                run_engine("sp", e)


V_C = 0
V_ADAB = V_C + 8
V_LNG = V_ADAB + 192
V_LNB = V_LNG + 64
V_SUBG = V_LNB + 64
V_PASTNEG = V_SUBG + 2
V_NEGPAST = V_PASTNEG + 128
V_ONE = V_NEGPAST + 128
V_ZERO = V_ONE + 8
V_EPS = V_ZERO + 8
V_LAM = V_EPS + 1
NV = V_LAM + 512


STAGE = 99
SUB = 99
NPAIRS = 8
ROPE = 1
NODYN = 0
DBG_DEST = 0


def build(n_layers=DEPTH):
    nc = bass.Bass("TRN2", target_bir_lowering=False)

    def dr(name, shape, kind="ExternalInput"):
        return nc.dram_tensor(name, shape, F32, kind=kind).ap()

    xT_d = dr("xT", [128, 8, S])
    vec_d = dr("vec", [128, NV])
    cs_d = dr("cs", [128, 2, S])
    msk_d = dr("msk", [128, 8, 512])
    cst_d = dr("cst", [128, 3, 128])
    id4_d = dr("id4", [128, 512])
    nmo, ndi = (n_layers + 1) // 2, max(1, n_layers // 2)
    ada_w_d = dr("ada_w", [n_layers, D, 6 * D])
    mwin_d = dr("moba_w_in", [nmo, D, 3 * D])
    mwout_d = dr("moba_w_out", [nmo, D, D])
    dwin_d = dr("diff_w_in", [ndi, D, 3 * D])
    dwout_d = dr("diff_w_out", [ndi, D, D])
    wup_d = dr("mlp_w_up", [n_layers, D, DFF])
    wdn_d = dr("mlp_w_down", [n_layers, DFF, D])
    out_d = dr("outT", [128, 8, S], kind="ExternalOutput")

    st = contextlib.ExitStack()
    with st:
        def sb(name, shape, dt):
            return st.enter_context(nc.sbuf_tensor("sb_" + name, shape, dt))

        X = sb("X", [128, 8, S], F32)
        hT = sb("hT", [128, 8, S], BF16)
        CS = sb("CS", [128, 2, S], F32)
        MSK = sb("MSK", [128, 8, 512], BF16)
        CST = sb("CST", [128, 3, 128], BF16)
        vec = sb("vec", [128, NV], F32)
        mod = sb("mod", [128, DEPTH * 48], F32)
        dv = sb("dv", [128, 2 * DEPTH, 6, 8], F32)
        scb = sb("scb", [128, 8], BF16)
        lamv = sb("lamv", [128, 16], F32)
        lamt = sb("lamt", [128, 2, 64], F32)
        t1 = [sb("t1_0", [128, 512], F32)] * 2
        t2 = [sb("t2_0", [128, 512], F32)] * 2
        fin = [sb("fin_%d" % i, [128, 512], F32) for i in range(2)]
        lnx = [sb("lnx_%d" % i, [128, 512], BF16) for i in range(2)]
        lnq = [sb("lnq_%d" % i, [128, 512], BF16) for i in range(2)]
        gsm = sb("gsm", [128, 96], F32)
        gmA = sb("gmA", [128, 16, 8], F32)
        topA = sb("topA", [128, 16, 8], F32)
        notA = sb("notA", [128, 16, 8], F32)
        kmb = sb("kmb", [128, 128], BF16)
        ID4 = sb("ID4", [128, 512], BF16)
        bq3 = sb("bq3", [128, 8, 16], F32)
        Wada = sb("Wada", [128, 8, 256], BF16)
        ARENA = 29760
        arena = sb("arena", [128, ARENA], BF16)
        ps = [st.enter_context(nc.psum_tensor("ps%d" % i, [128, 512], F32)) for i in range(8)]

        off = [0]

        def carve(n):
            a = off[0]
            off[0] += n
            return a

        def view2(a, n):
            return arena[:, a:a + n]

        def view3(a, n0, n1):
            return arena[:, a:a + n0 * n1].rearrange("p (a b) -> p a b", a=n0)

        Wp = [view3(carve(3072), 8, 384) for _ in range(2)]
        Wout = [view2(carve(1024), 1024) for _ in range(2)]
        qT = [view2(carve(2048), 2048) for _ in range(2)]
        kT = [view2(carve(2048), 2048) for _ in range(2)]
        Vp = [view3(carve(2048), 16, 128) for _ in range(2)]
        oT = [view2(carve(2048), 2048) for _ in range(2)]
        qsb = [view2(carve(512), 512) for _ in range(2)]
        PT = [view2(carve(512), 512) for _ in range(4)]
        Dt = [view2(carve(512), 512) for _ in range(4)]
        assert off[0] <= ARENA
        off[0] = 0
        Wup = [view3(carve(4096), 8, 512) for _ in range(2)]
        Wdn = [view3(carve(4096), 4, 1024) for _ in range(2)]
        uT = [view3(carve(2048), 4, 512) for _ in range(2)]

        ident = CST[:, 0, :]
        ones = CST[:, 1, :]
        Pm = CST[:, 2, :]
        Ctab = CS[:, 0, :]
        Stab = CS[:, 1, :]

        P = Prog(nc)
        gb = {"i": 0, "n": 2}

        def gbank():
            b = gb["i"] % gb["n"]
            gb["i"] += 1
            return b

        def trs(tr):
            return slice(tr * 512, (tr + 1) * 512)

        def pk(bank):
            return [("ps", bank, 0), ("ps", bank, 1)]

        def barrier():
            P.dve(lambda e: e.memset(gsm[:, 90:91], 0.0), w=["arena"])

        for c in range(8):
            P.dma("sp", lambda e, c=c: e.dma_start(out=X[:, c, :], in_=xT_d[:, c, :]),
                  w=[("X", c, tr) for tr in range(4)])
        P.dma("sp", lambda e: e.dma_start(out=vec[:], in_=vec_d), w=["vec"])
        P.dma("sp", lambda e: e.dma_start(out=CS[:], in_=cs_d), w=["CS"])
        P.dma("pool", lambda e: e.dma_start(out=MSK[:], in_=msk_d), w=["MSK"])
        P.dma("pool", lambda e: e.dma_start(out=CST[:], in_=cst_d), w=["CST"])
        P.dma("pool", lambda e: e.dma_start(out=ID4[:], in_=id4_d), w=["ID4"])
        P.dve(lambda e: e.memset(kmb[:], 0.0), w=["kmb"])
        P.act(lambda e: e.activation(out=scb[:], in_=vec[:, V_C:V_C + 8], func=AF.Silu), r=["vec"], w=["scb"])

        def adaln_chunk(l, n):
            bank = 7
            awv = ada_w_d[l].rearrange("(kc p) n -> p kc n", p=128)
            P.dma("pool", lambda e: e.dma_start(out=Wada[:], in_=awv[:, :, n * 256:(n + 1) * 256]), w=["wada"])
            for j in range(2):
                col = n * 2 + j
                for kc in range(8):
                    last = (j == 1 and kc == 7)
                    P.pe(lambda e, j=j, kc=kc, col=col: e.matmul(
                        ps[bank][:, col:col + 1], lhsT=Wada[:, kc, j * 128:(j + 1) * 128],
                        rhs=scb[:, kc:kc + 1], start=(kc == 0), stop=(kc == 7)),
                        r=(["wada", "scb"] if last else ["scb"]), r2=([] if last else ["wada"]),
                        w=[("adaps", col)] + (pk(bank) if n == 0 and j == 0 and kc == 0 else []))
            if n == 23:
                P.dve(lambda e: e.tensor_tensor(out=mod[:, l * 48:(l + 1) * 48], in0=ps[bank][:, 0:48],
                                                in1=vec[:, V_ADAB + l * 48:V_ADAB + (l + 1) * 48], op=ALU.add),
                      r=pk(bank) + [("adaps", c2) for c2 in range(48)] + ["vec"], w=[("mod", l)] + pk(bank))

        def adaln(l):
            for n in range(24):
                adaln_chunk(l, n)

        def derive(l, s):
            k = l * 2 + s
            base = l * 48 + s * 24
            shift = mod[:, base:base + 8]
            scale = mod[:, base + 8:base + 16]
            gate = mod[:, base + 16:base + 24]
            if k == 0:
                gp = vec[:, V_ONE:V_ONE + 8]
                bp = vec[:, V_ZERO:V_ZERO + 8]
            else:
                gp = vec[:, V_LNG + (k - 1) * 8:V_LNG + k * 8]
                bp = vec[:, V_LNB + (k - 1) * 8:V_LNB + k * 8]
            key = ("dv", k)
            rr = [("mod", l), "vec"]
            P.dve(lambda e: e.tensor_single_scalar(out=dv[:, k, 0, :], in_=scale, scalar=1.0, op=ALU.add), r=rr, w=[(key, 0)])
            P.dve(lambda e: e.tensor_tensor(out=dv[:, k, 1, :], in0=gp, in1=dv[:, k, 0, :], op=ALU.mult), r=rr + [(key, 0)], w=[(key, 1)])
            P.dve(lambda e: e.tensor_tensor(out=dv[:, k, 2, :], in0=bp, in1=dv[:, k, 0, :], op=ALU.mult), r=rr + [(key, 0)], w=[(key, 2)])
            P.dve(lambda e: e.tensor_tensor(out=dv[:, k, 2, :], in0=dv[:, k, 2, :], in1=shift, op=ALU.add), r=rr + [(key, 2)], w=[(key, 2)])
            P.dve(lambda e: e.tensor_single_scalar(out=dv[:, k, 3, :], in_=gp, scalar=ALPHA, op=ALU.mult), r=rr, w=[(key, 3)])
            P.dve(lambda e: e.tensor_single_scalar(out=dv[:, k, 4, :], in_=bp, scalar=ALPHA, op=ALU.mult), r=rr, w=[(key, 4)])
            P.dve(lambda e: e.tensor_single_scalar(out=dv[:, k, 5, :], in_=gate, scalar=1.0, op=ALU.add), r=rr, w=[(key, 5)])

        def make_h(k):
            key = ("dv", k)
            for c in range(8):
                for tr in range(4):
                    P.act(lambda e, c=c, tr=tr: e.activation(
                        out=hT[:, c, trs(tr)], in_=X[:, c, trs(tr)], func=AF.Identity,
                        bias=dv[:, k, 2, c:c + 1], scale=dv[:, k, 1, c:c + 1]),
                        r=[("X", c, tr), (key, 1), (key, 2)], w=[("h", c, tr)])
                    P.dve(lambda e, c=c, tr=tr: e.tensor_scalar(
                        out=X[:, c, trs(tr)], in0=X[:, c, trs(tr)], scalar1=dv[:, k, 3, c:c + 1],
                        scalar2=dv[:, k, 4, c:c + 1], op0=ALU.mult, op1=ALU.add),
                        r=[("X", c, tr), (key, 3), (key, 4)], w=[("X", c, tr)])

        def layer_norm():
            gb["n"] = 2
            for tr in range(4):
                bm, bq = 2 + (tr % 2) * 2, 3 + (tr % 2) * 2
                for c in range(8):
                    i = c % 2
                    P.act(lambda e, c=c, tr=tr, i=i: e.copy(out=lnx[i][:], in_=X[:, c, trs(tr)]),
                          r=[("X", c, tr)], w=[("lnx", i)])
                    P.dve(lambda e, c=c, tr=tr, i=i: e.tensor_tensor(out=lnq[i][:], in0=X[:, c, trs(tr)],
                                                                    in1=X[:, c, trs(tr)], op=ALU.mult),
                          r=[("X", c, tr)], w=[("lnq", i)])
                    P.pe(lambda e, c=c, i=i, bm=bm: e.matmul(ps[bm][:], lhsT=ones, rhs=lnx[i][:], start=(c == 0), stop=(c == 7)),
                         r=[("lnx", i), "CST"], w=pk(bm))
                    P.pe(lambda e, c=c, i=i, bq=bq: e.matmul(ps[bq][:], lhsT=ones, rhs=lnq[i][:], start=(c == 0), stop=(c == 7)),
                         r=[("lnq", i), "CST"], w=pk(bq))
                mu, rs = fin[0], fin[1]
                P.dve(lambda e, bm=bm: e.tensor_single_scalar(out=mu[:], in_=ps[bm][:], scalar=1.0 / D, op=ALU.mult),
                      r=pk(bm), w=[("fin", 0)])
                P.dve(lambda e: e.tensor_tensor(out=rs[:], in0=mu[:], in1=mu[:], op=ALU.mult), r=[("fin", 0)], w=[("fin", 1)])
                P.dve(lambda e, bq=bq: e.scalar_tensor_tensor(out=rs[:], in0=ps[bq][:], scalar=1.0 / D, in1=rs[:],
                                                             op0=ALU.mult, op1=ALU.subtract),
                      r=pk(bq) + [("fin", 1)], w=[("fin", 1)])
                P.act(lambda e: e.activation(out=rs[:], in_=rs[:], func=AF.Ln, bias=vec[:, V_EPS:V_EPS + 1], scale=1.0),
                      r=[("fin", 1), "vec"], w=[("fin", 1)])
                P.act(lambda e: e.activation(out=rs[:], in_=rs[:], func=AF.Exp, scale=-0.5), r=[("fin", 1)], w=[("fin", 1)])
                for c in range(8):
                    P.dve(lambda e, c=c, tr=tr: e.tensor_tensor(out=X[:, c, trs(tr)], in0=X[:, c, trs(tr)], in1=mu[:],
                                                               op=ALU.subtract),
                          r=[("X", c, tr), ("fin", 0)], w=[("X", c, tr)])
                    P.dve(lambda e, c=c, tr=tr: e.tensor_tensor(out=X[:, c, trs(tr)], in0=X[:, c, trs(tr)], in1=rs[:],
                                                               op=ALU.mult),
                          r=[("X", c, tr), ("fin", 1)], w=[("X", c, tr)])

        def attention(l, kind):
            k = l * 2
            j = l // 2
            gb["n"] = 2
            win_d = (mwin_d if kind == "moba" else dwin_d)[j].rearrange("(kc p) n -> p kc n", p=128)
            wout_d = (mwout_d if kind == "moba" else dwout_d)[j]
            GT = dv[:, k, 5, :]
            if kind == "diff":
                lam_init = 0.8 - 0.6 * math.exp(-0.3 * l)
                lb = V_LAM + j * 256
                for i2 in range(2):
                    P.dve(lambda e, i2=i2: e.tensor_tensor(out=lamt[:, i2, :], in0=vec[:, lb + i2 * 128:lb + i2 * 128 + 64],
                                                          in1=vec[:, lb + i2 * 128 + 64:lb + i2 * 128 + 128], op=ALU.mult),
                          r=["vec"], w=[("lamt", i2)])
                P.dve(lambda e: e.tensor_reduce(out=lamv[:, 2:4], in_=lamt[:], axis=AX.X, op=ALU.add),
                      r=[("lamt", 0), ("lamt", 1)], w=["lamv23"])
                P.act(lambda e: e.activation(out=lamv[:, 4:6], in_=lamv[:, 2:4], func=AF.Exp), r=["lamv23"], w=["lamv45"])
                P.dve(lambda e: e.scalar_tensor_tensor(out=lamv[:, 0:1], in0=lamv[:, 5:6], scalar=-lam_init, in1=lamv[:, 4:5],
                                                       op0=ALU.add, op1=ALU.subtract),
                      r=["lamv45"], w=["neglam"])
                P.dve(lambda e: e.tensor_single_scalar(out=lamv[:, 1:2], in_=vec[:, V_SUBG + j:V_SUBG + j + 1],
                                                       scalar=(1.0 - lam_init), op=ALU.mult),
                      r=["vec"], w=["gs"])

            pending = []

            def flush(force=False):
                keep = []
                for item in pending:
                    item[0] -= 1
                    if force or item[0] <= 0:
                        item[1]()
                    else:
                        keep.append(item)
                pending[:] = keep

            def do_pair(p):
                b = p % 2
                for which in range(3):
                    P.dma("pool", lambda e, b=b, which=which, p=p: e.dma_start(
                        out=Wp[b][:, :, which * 128:(which + 1) * 128],
                        in_=win_d[:, :, which * 1024 + p * 128:which * 1024 + (p + 1) * 128]),
                        r=["arena"], w=[("wp", b, which)])
                P.dma("pool", lambda e, b=b, p=p: e.dma_start(out=Wout[b], in_=wout_d[p * 128:(p + 1) * 128, :]),
                      r=["arena"], w=[("wout", b)])
                if SUB < 1:
                    return
                for which, dst, dkey in ((0, qT[b], "qT"), (1, kT[b], "kT")):
                    for tr in range(4):
                        bank = gbank()
                        for kc in range(8):
                            P.pe(lambda e, b=b, which=which, tr=tr, kc=kc, bank=bank: e.matmul(
                                ps[bank][:], lhsT=Wp[b][:, kc, which * 128:(which + 1) * 128], rhs=hT[:, kc, trs(tr)],
                                start=(kc == 0), stop=(kc == 7)),
                                r=[("wp", b, which), ("h", kc, tr), "arena"], w=pk(bank))
                        i = tr % 2
                        P.act(lambda e, i=i, bank=bank: e.copy(out=qsb[i], in_=ps[bank][:]),
                              r=pk(bank) + ["arena"], w=[("qsb", i)])
                        if not ROPE:
                            P.act(lambda e, bank=bank, dst=dst, tr=tr: e.copy(out=dst[:, trs(tr)], in_=ps[bank][:]),
                                  r=pk(bank) + ["arena"], w=[(dkey, b, tr)])
                            continue
                        bank2 = gbank()
                        P.pe(lambda e, i=i, bank2=bank2: e.matmul(ps[bank2][:], lhsT=Pm, rhs=qsb[i], start=True, stop=True),
                             r=[("qsb", i), "CST"], w=pk(bank2))
                        P.dve(lambda e, i=i, bank=bank, tr=tr: e.tensor_tensor(out=t1[i][:], in0=ps[bank][:], in1=Ctab[:, trs(tr)],
                                                                              op=ALU.mult),
                              r=pk(bank) + ["CS", ("qsb", i)], w=[("t1", 0)])
                        P.dve(lambda e, i=i, bank2=bank2, tr=tr: e.tensor_tensor(out=t2[i][:], in0=ps[bank2][:],
                                                                                in1=Stab[:, trs(tr)], op=ALU.mult),
                              r=pk(bank2) + ["CS"], w=[("t2", 0)])
                        P.dve(lambda e, i=i, dst=dst, tr=tr: e.tensor_tensor(out=dst[:, trs(tr)], in0=t1[i][:], in1=t2[i][:],
                                                                            op=ALU.add),
                              r=[("t1", 0), ("t2", 0), "arena"], w=[(dkey, b, tr)])
                if SUB < 2:
                    return
                for g4 in range(4):
                    bank = gbank()
                    for tt4 in range(4):
                        tt = g4 * 4 + tt4
                        for kc in range(8):
                            P.pe(lambda e, b=b, tt=tt, tt4=tt4, kc=kc, bank=bank: e.matmul(
                                ps[bank][:, tt4 * 128:(tt4 + 1) * 128], lhsT=hT[:, kc, tt * 128:(tt + 1) * 128],
                                rhs=Wp[b][:, kc, 256:384], start=(kc == 0), stop=(kc == 7)),
                                r=[("wp", b, 2), ("h", kc, g4), "arena"], w=pk(bank))
                    for tt4 in range(4):
                        P.act(lambda e, b=b, g4=g4, tt4=tt4, bank=bank: e.copy(
                            out=Vp[b][:, g4 * 4 + tt4, :], in_=ps[bank][:, tt4 * 128:(tt4 + 1) * 128]),
                            r=pk(bank) + ["arena"], w=[("V", b, g4, tt4)])
                flush(force=True)
                if SUB < 3:
                    return
                if kind == "moba" and not NODYN:
                    P.dve(lambda e, b=b: e.tensor_reduce(out=gsm[:, 48:56], in_=kT[b].rearrange("p (a b) -> p a b", a=8),
                                                         axis=AX.X, op=ALU.add),
                          r=[("kT", b, tr) for tr in range(4)], w=["kms"])
                    P.dve(lambda e: e.tensor_single_scalar(out=kmb[0:64, 0:8], in_=gsm[0:64, 48:56], scalar=1.0 / 256, op=ALU.mult),
                          r=["kms"], w=["kmb"])
                    P.dve(lambda e: e.tensor_single_scalar(out=kmb[64:128, 8:16], in_=gsm[64:128, 48:56], scalar=1.0 / 256,
                                                           op=ALU.mult),
                          r=["kms", "kmb"], w=["kmb"])
                    for qt in range(8):
                        gbk = 4 + qt // 4
                        P.pe(lambda e, b=b, qt=qt, gbk=gbk: e.matmul(
                            ps[gbk][:, (qt % 4) * 128:(qt % 4 + 1) * 128],
                            lhsT=qT[b][:, (8 + qt) * 128:(9 + qt) * 128], rhs=kmb[:], start=True, stop=True),
                            r=[("qT", b, (8 + qt) // 4), "kmb", "arena"], w=pk(gbk))
                    for g in range(16):
                        gbk = 4 + (g // 2) // 4
                        P.dve(lambda e, g=g, gbk=gbk: e.tensor_tensor(
                            out=gmA[:, g, :], in0=ps[gbk][:, ((g // 2) % 4) * 128 + (g % 2) * 8:((g // 2) % 4) * 128 + (g % 2) * 8 + 8],
                            in1=vec[:, V_PASTNEG + g * 8:V_PASTNEG + g * 8 + 8], op=ALU.add),
                            r=pk(gbk) + ["vec"], w=[("gm", g)])
                        P.dve(lambda e, g=g: e.max(out=topA[:, g, :], in_=gmA[:, g, :]), r=[("gm", g)], w=[("top8", g)])
                        P.dve(lambda e, g=g: e.tensor_scalar(out=notA[:, g, :], in0=gmA[:, g, :], scalar1=topA[:, g, 2:3],
                                                             scalar2=None, op0=ALU.is_lt),
                              r=[("gm", g), ("top8", g)], w=[("nots", g)])
                    P.dve(lambda e: e.tensor_tensor(out=bq3[:].rearrange("p a b -> p (a b)"),
                                                    in0=notA[:].rearrange("p a b -> p (a b)"),
                                                    in1=vec[:, V_NEGPAST:V_NEGPAST + 128], op=ALU.mult),
                          r=[("nots", g) for g in range(16)] + ["vec"], w=[("bq3", qt) for qt in range(8)])
                if SUB == 3.5:
                    bankd = 6
                    P.pe(lambda e, bankd=bankd: e.matmul(ps[bankd][:, 0:128], lhsT=ident, rhs=ones, start=True, stop=True),
                         r=["CST"] + ([("bq3", 0)] if DBG_DEST == 0 else []), w=pk(bankd))
                    P.dve(lambda e: e.memset(gsm[:, 91:92], 0.0), r=pk(bankd), w=[("X", c, tr) for c in range(8) for tr in range(4)])
                if SUB < 4:
                    return

                def emit_pv(qr, kt, hh, pti, first, last):
                    if kind == "moba":
                        ob, sbk = (4, 5) if qr % 2 == 0 else (6, 7)
                        rows = slice(hh * 64, (hh + 1) * 64)
                        P.pe(lambda e: e.matmul(ps[ob][rows, :], lhsT=Vp[b][:, kt, hh * 64:(hh + 1) * 64], rhs=PT[pti],
                                                start=first, stop=last),
                             r=[("V", b, kt // 4, kt % 4), ("PT", pti), "arena"], w=[("ps", ob, hh)])
                        P.pe(lambda e: e.matmul(ps[sbk][rows, :], lhsT=ones[:, 0:64], rhs=PT[pti], start=first, stop=last),
                             r=[("PT", pti), "CST"], w=[("ps", sbk, hh)])
                    else:
                        ob, sbk = 4 + 2 * hh, 5 + 2 * hh
                        P.pe(lambda e: e.matmul(ps[ob][:], lhsT=Vp[b][:, kt, :], rhs=PT[pti], start=first, stop=last),
                             r=[("V", b, kt // 4, kt % 4), ("PT", pti), "arena"], w=pk(ob))
                        P.pe(lambda e: e.matmul(ps[sbk][:], lhsT=ones, rhs=PT[pti], start=first, stop=last),
                             r=[("PT", pti), "CST"], w=pk(sbk))

                def finalize(qr):
                    tr = qr
                    if kind == "moba":
                        ob, sbk = (4, 5) if qr % 2 == 0 else (6, 7)
                        f = fin[qr % 2]
                        P.dve(lambda e: e.reciprocal(out=f[:], in_=ps[sbk][:]),
                              r=pk(sbk), w=[("fin", qr % 2)])
                        P.dve(lambda e: e.tensor_tensor(out=oT[b][:, trs(tr)], in0=ps[ob][:], in1=f[:], op=ALU.mult),
                              r=pk(ob) + [("fin", qr % 2), "arena"], w=[("oT", b, tr)])
                    else:
                        f0, f1 = fin[0], fin[1]
                        P.dve(lambda e: e.reciprocal(out=f0[:], in_=ps[5][:]), r=pk(5), w=[("fin", 0)])
                        P.dve(lambda e: e.tensor_tensor(out=f0[:], in0=ps[4][:], in1=f0[:], op=ALU.mult),
                              r=pk(4) + [("fin", 0)], w=[("fin", 0)])
                        P.dve(lambda e: e.reciprocal(out=f1[:], in_=ps[7][:]), r=pk(7), w=[("fin", 1)])
                        P.dve(lambda e: e.tensor_tensor(out=f1[:], in0=ps[6][:], in1=f1[:], op=ALU.mult),
                              r=pk(6) + [("fin", 1)], w=[("fin", 1)])
                        P.dve(lambda e: e.scalar_tensor_tensor(out=f0[:], in0=f1[:], scalar=lamv[:, 0:1], in1=f0[:],
                                                               op0=ALU.mult, op1=ALU.add),
                              r=[("fin", 0), ("fin", 1), "neglam"], w=[("fin", 0)])
                        P.dve(lambda e: e.tensor_tensor(out=lnq[0][:], in0=f0[:], in1=f0[:], op=ALU.mult),
                              r=[("fin", 0)], w=[("lnq", 0)])
                        bank = gbank()
                        P.pe(lambda e: e.matmul(ps[bank][:], lhsT=ones, rhs=lnq[0][:], start=True, stop=True),
                             r=[("lnq", 0), "CST"], w=pk(bank))
                        P.act(lambda e: e.activation(out=f1[:], in_=ps[bank][:], func=AF.Ln, bias=vec[:, V_EPS:V_EPS + 1],
                                                     scale=1.0 / 128),
                              r=pk(bank) + ["vec", ("fin", 1)], w=[("fin", 1)])
                        P.act(lambda e: e.activation(out=f1[:], in_=f1[:], func=AF.Exp, scale=-0.5),
                              r=[("fin", 1)], w=[("fin", 1)])
                        P.dve(lambda e: e.scalar_tensor_tensor(out=oT[b][:, trs(tr)], in0=f0[:], scalar=lamv[:, 1:2], in1=f1[:],
                                                               op0=ALU.mult, op1=ALU.mult),
                              r=[("fin", 0), ("fin", 1), "gs", "arena"], w=[("oT", b, tr)])

                def outproj(qr):
                    tr = qr
                    if SUB < 5:
                        return
                    for oc in range(8):
                        bank = gbank()
                        P.pe(lambda e, oc=oc, bank=bank: e.matmul(ps[bank][:], lhsT=Wout[b][:, oc * 128:(oc + 1) * 128],
                                                                  rhs=oT[b][:, trs(tr)], start=True, stop=True),
                             r=[("wout", b), ("oT", b, tr), "arena"], w=pk(bank))
                        P.dve(lambda e, oc=oc, bank=bank: e.scalar_tensor_tensor(
                            out=X[:, oc, trs(tr)], in0=ps[bank][:], scalar=GT[:, oc:oc + 1], in1=X[:, oc, trs(tr)],
                            op0=ALU.mult, op1=ALU.add),
                            r=pk(bank) + [("X", oc, tr), (("dv", k), 5)], w=[("X", oc, tr)])

                prev = []
                pti_ctr = [0]
                for qr in range(4):
                    nkt = 4 * qr + 4
                    for kt in range(nkt):
                        cur = []
                        for hh in range(2):
                            sbank = 2 + hh
                            pti = pti_ctr[0] % 4
                            pti_ctr[0] += 1
                            jd = kt - 4 * qr
                            extra = []
                            if jd >= 0:
                                extra.append(("static", (0 if kind == "moba" else 4) + jd))
                            if kind == "moba" and not NODYN and qr >= 2 and (kt // 2) < 2 * qr + 1:
                                nblk = kt // 2
                                slot = hh * 2 + (nblk % 2)
                                if kt % 2 == 0:
                                    col = hh * 8 + nblk
                                    for q4 in range(4):
                                        P.dve(lambda e, slot=slot, col=col, qr=qr, q4=q4: e.tensor_scalar(
                                            out=Dt[slot][:, q4 * 128:(q4 + 1) * 128], in0=ident,
                                            scalar1=bq3[:, (qr - 2) * 4 + q4, col:col + 1], scalar2=None, op0=ALU.mult),
                                            r=[("bq3", (qr - 2) * 4 + q4), "CST", "arena"], w=[("D", slot, q4)])
                                extra.append(("dyn", slot))
                            nmm = 1 + len(extra)
                            P.pe(lambda e, hh=hh, kt=kt, qr=qr, sbank=sbank, nmm=nmm: e.matmul(
                                ps[sbank][:], lhsT=kT[b][hh * 64:(hh + 1) * 64, kt * 128:(kt + 1) * 128],
                                rhs=qT[b][hh * 64:(hh + 1) * 64, trs(qr)], start=True, stop=(nmm == 1)),
                                r=[("kT", b, kt // 4), ("qT", b, qr), "arena"], w=pk(sbank))
                            for xi, (xk, xv) in enumerate(extra):
                                lastx = (xi == len(extra) - 1)
                                if xk == "static":
                                    P.pe(lambda e, xv=xv, sbank=sbank, lastx=lastx: e.matmul(
                                        ps[sbank][:], lhsT=ident, rhs=MSK[:, xv, :], start=False, stop=lastx),
                                        r=["MSK", "CST"], w=pk(sbank))
                                else:
                                    P.pe(lambda e, xv=xv, sbank=sbank, lastx=lastx: e.matmul(
                                        ps[sbank][:], lhsT=ones, rhs=Dt[xv], start=False, stop=lastx),
                                        r=[("D", xv, q4) for q4 in range(4)] + ["CST", "arena"], w=pk(sbank))
                            P.act(lambda e, sbank=sbank, pti=pti: e.activation(out=PT[pti], in_=ps[sbank][:], func=AF.Exp,
                                                                                scale=0.125),
                                  r=pk(sbank) + ["arena"], w=[("PT", pti)])
                            cur.append((qr, kt, hh, pti, kt == 0, kt == nkt - 1))
                        for a in prev:
                            emit_pv(*a)
                        if prev and prev[0][5]:
                            pqr = prev[0][0]
                            finalize(pqr)
                            pending.append([2, (lambda pqr=pqr: outproj(pqr))])
                        prev = cur
                        flush()
                for a in prev:
                    emit_pv(*a)
                finalize(3)
                pending.append([1, (lambda: outproj(3))])

            for p in range(NPAIRS):
                do_pair(p)
            flush(force=True)

        def mlp(l, nxt_ada=None):
            k = l * 2 + 1
            GT = dv[:, k, 5, :]
            gb["n"] = 2
            upv = wup_d[l].rearrange("(kc p) n -> p kc n", p=128)
            dnv = wdn_d[l].rearrange("(fc p) n -> p fc n", p=128)
            ubank = [0]
            ybank = [0]

            def up(g, tr, ui):
                b = g % 2
                for fc in range(4):
                    bank = ubank[0] % 4
                    ubank[0] += 1
                    for kc in range(8):
                        P.pe(lambda e, fc=fc, kc=kc, bank=bank: e.matmul(
                            ps[bank][:], lhsT=Wup[b][:, kc, fc * 128:(fc + 1) * 128], rhs=hT[:, kc, trs(tr)],
                            start=(kc == 0), stop=(kc == 7)),
                            r=[("wup", b), ("h", kc, tr), "arena"], w=pk(bank))
                    ti = fc % 2
                    P.act(lambda e, bank=bank, ti=ti: e.activation(out=t1[ti][:], in_=ps[bank][:], func=AF.Relu),
                          r=pk(bank), w=[("t1", 0)])
                    P.dve(lambda e, fc=fc, ti=ti: e.tensor_tensor(out=uT[ui][:, fc, :], in0=t1[ti][:], in1=t1[ti][:], op=ALU.mult),
                          r=[("t1", 0), "arena"], w=[("uT", ui, fc)])

            def down(g, tr, ui):
                b = g % 2
                for oc in range(8):
                    bank = 4 + ybank[0] % 3
                    ybank[0] += 1
                    for fc in range(4):
                        P.pe(lambda e, fc=fc, oc=oc, bank=bank: e.matmul(
                            ps[bank][:], lhsT=Wdn[b][:, fc, oc * 128:(oc + 1) * 128], rhs=uT[ui][:, fc, :],
                            start=(fc == 0), stop=(fc == 3)),
                            r=[("wdn", b), ("uT", ui, fc), "arena"], w=pk(bank))
                    P.dve(lambda e, oc=oc, bank=bank: e.scalar_tensor_tensor(
                        out=X[:, oc, trs(tr)], in0=ps[bank][:], scalar=GT[:, oc:oc + 1], in1=X[:, oc, trs(tr)],
                        op0=ALU.mult, op1=ALU.add),
                        r=pk(bank) + [("X", oc, tr), (("dv", k), 5)], w=[("X", oc, tr)])

            prev = None
            step = 0
            for g in range(8):
                b = g % 2
                P.dma("pool", lambda e, b=b, g=g: e.dma_start(out=Wup[b], in_=upv[:, :, g * 512:(g + 1) * 512]),
                      r=["arena"], w=[("wup", b)])
                P.dma("pool", lambda e, b=b, g=g: e.dma_start(out=Wdn[b], in_=dnv[:, g * 4:(g + 1) * 4, :]),
                      r=["arena"], w=[("wdn", b)])
                for tr in range(4):
                    ui = step % 2
                    step += 1
                    up(g, tr, ui)
                    if prev is not None:
                        down(*prev)
                    prev = (g, tr, ui)
                    if nxt_ada is not None and step <= 24:
                        adaln_chunk(nxt_ada, step - 1)
            down(*prev)

        adaln(0)
        for l in range(n_layers):
            kind = "moba" if l % 2 == 0 else "diff"
            derive(l, 0)
            derive(l, 1)
            if STAGE < 2:
                break
            barrier()
            make_h(2 * l)
            if STAGE < 3:
                break
            attention(l, kind)
            if STAGE < 4:
                break
            layer_norm()
            if STAGE < 5:
                break
            barrier()
            make_h(2 * l + 1)
            if STAGE < 6:
                break
            mlp(l, (l + 1) if l + 1 < n_layers else None)
            if STAGE < 7:
                break
            layer_norm()
        kl = 2 * n_layers - 1
        outs = []
        for c in range(8):
            for tr in range(4):
                P.dve(lambda e, c=c, tr=tr: e.tensor_scalar(
                    out=X[:, c, trs(tr)], in0=X[:, c, trs(tr)], scalar1=vec[:, V_LNG + kl * 8 + c:V_LNG + kl * 8 + c + 1],
                    scalar2=vec[:, V_LNB + kl * 8 + c:V_LNB + kl * 8 + c + 1], op0=ALU.mult, op1=ALU.add),
                    r=[("X", c, tr), "vec"], w=[("X", c, tr)])
            outs.append(P.dma("sp", lambda e, c=c: e.dma_start(out=out_d[:, c, :], in_=X[:, c, :]),
                              r=[("X", c, tr) for tr in range(4)]))
        P.emit(final_wait_ops=outs)
    return nc


def _consts():
    pos = np.arange(S, dtype=np.float32)
    inv = (np.float32(ROPE_THETA) ** (-np.arange(0, 16, 2, dtype=np.float32) / np.float32(16))).astype(np.float32)
    ang = (pos[:, None] * inv[None, :]).astype(np.float32)
    cos = np.cos(ang).astype(np.float32)
    sin = np.sin(ang).astype(np.float32)
    cs = np.zeros((128, 2, S), np.float32)
    cs[:, 0, :] = 1.0
    for hh in range(2):
        for d in range(8):
            cs[hh * 64 + d, 0, :] = cos[:, d]
            cs[hh * 64 + 8 + d, 0, :] = cos[:, d]
            cs[hh * 64 + d, 1, :] = -sin[:, d]
            cs[hh * 64 + 8 + d, 1, :] = sin[:, d]
    msk = np.zeros((128, 8, 512), np.float32)
    kk = np.arange(128)[:, None]
    qq = np.arange(512)[None, :]
    for j in range(4):
        kp = j * 128 + kk
        same0 = (kp < 256) & (qq < 256) & (kp <= qq)
        past = (kp < 256) & (qq >= 256)
        same1 = (kp >= 256) & (qq >= 256) & (kp <= qq)
        msk[:, j, :] = np.where(same0 | past | same1, 0.0, NEG)
        msk[:, 4 + j, :] = np.where(kp <= qq, 0.0, NEG)
    cst = np.zeros((128, 3, 128), np.float32)
    cst[:, 0, :] = np.eye(128, dtype=np.float32)
    cst[:, 1, :] = 1.0
    for hh in range(2):
        for d in range(8):
            cst[hh * 64 + d + 8, 2, hh * 64 + d] = 1.0
            cst[hh * 64 + d, 2, hh * 64 + d + 8] = 1.0
    return cs, msk, cst


def _vec(c_b, ada_b, ln_g, ln_b, subln_g, lams):
    v = np.zeros((128, NV), np.float32)
    v[:, V_C:V_C + 8] = c_b.reshape(8, 128).T
    v[:, V_ADAB:V_ADAB + 192] = ada_b.reshape(DEPTH, 48, 128).transpose(2, 0, 1).reshape(128, 192)
    v[:, V_LNG:V_LNG + 64] = ln_g.reshape(DEPTH * 2, 8, 128).transpose(2, 0, 1).reshape(128, 64)
    v[:, V_LNB:V_LNB + 64] = ln_b.reshape(DEPTH * 2, 8, 128).transpose(2, 0, 1).reshape(128, 64)
    v[:, V_SUBG:V_SUBG + 2] = subln_g.T
    for qt in range(8):
        qb = 4 + qt // 2
        for hh in range(2):
            for n in range(8):
                v[:, V_PASTNEG + qt * 16 + hh * 8 + n] = 0.0 if n < qb else -1e30
                v[:, V_NEGPAST + qt * 16 + hh * 8 + n] = NEG if n < qb else 0.0
    v[:, V_ONE:V_ONE + 8] = 1.0
    v[:, V_EPS] = LN_EPS
    for j in range(2):
        for i, a in enumerate(lams):
            v[:, V_LAM + j * 256 + i * 64:V_LAM + j * 256 + (i + 1) * 64] = a[j][None, :]
    return v


_NC_CACHE = {}
STAGE = 99


def _run(inputs, n_layers, cores):
    x = np.asarray(inputs["x"], np.float32)
    c = np.asarray(inputs["c"], np.float32)
    cs, msk, cst = _consts()
    f = lambda k: np.ascontiguousarray(np.asarray(inputs[k], np.float32))
    nmo, ndi = (n_layers + 1) // 2, max(1, n_layers // 2)
    nsl = {"ada_w": n_layers, "mlp_w_up": n_layers, "mlp_w_down": n_layers, "moba_w_in": nmo, "moba_w_out": nmo,
           "diff_w_in": ndi, "diff_w_out": ndi}
    shared = {k: np.ascontiguousarray(np.asarray(inputs[k], np.float32)[:n]) for k, n in nsl.items()}
    lams = [f("diff_lam_q1"), f("diff_lam_k1"), f("diff_lam_q2"), f("diff_lam_k2")]
    in_maps = []
    for b in cores:
        xT = np.ascontiguousarray(x[b].T.reshape(8, 128, S).transpose(1, 0, 2))
        m = {"xT": xT, "vec": _vec(c[b], f("ada_b"), f("ln_g"), f("ln_b"), f("diff_subln_g"), lams),
             "cs": cs, "msk": msk, "cst": cst, "id4": np.tile(np.eye(128, dtype=np.float32), (1, 4))}
        m.update(shared)
        in_maps.append(m)
    if n_layers not in _NC_CACHE:
        _NC_CACHE[n_layers] = build(n_layers)
    nc = _NC_CACHE[n_layers]
    res = run_bass_kernel_spmd(nc, in_maps, core_ids=list(range(len(cores))))
    outs = []
    for r in res.results:
        oT = r["outT"]
        outs.append(oT.transpose(2, 1, 0).reshape(S, D))
    return np.stack(outs, 0).astype(np.float32)


def kernel(**inputs):
    return _run(inputs, DEPTH, list(range(8)))
```
